# Optimizing a Trainium2 kernel written in Bass

```python
import math
import jax, jax.numpy as jnp
from jax import lax
import numpy as np

D_MODEL = 1024
BATCH = 4
SEQ = 8192
DEPTH = 1

N_HEADS = 8
HEAD_DIM = 64
ATTN_WIDTH = N_HEADS * HEAD_DIM
DILATIONS = ((128, 1), (512, 4), (2048, 16))
BLK = 128
ROPE_THETA = 10000.0
GMLP_GROUPS = 4
GMLP_WIDTH = D_MODEL // 2
GMLP_GROUP_DIM = GMLP_WIDTH // GMLP_GROUPS
CHUNK = 128
IN_COLS = 4 * ATTN_WIDTH + 3 * GMLP_WIDTH
MIX_WIDTH = ATTN_WIDTH + GMLP_WIDTH
PLE_DIM = 256
EPS = 1e-6
NEG = -1e30

kernel_name = "hybrid_dilated_attn_gmlp_parallel_heads"


def _rmsnorm(x, g):
    xf = x.astype(jnp.float32)
    y = xf * lax.rsqrt(jnp.mean(xf * xf, axis=-1, keepdims=True) + EPS)
    return (y * g.astype(jnp.float32)).astype(x.dtype)


def _layernorm(x, g, b):
    xf = x.astype(jnp.float32)
    mu = jnp.mean(xf, axis=-1, keepdims=True)
    xc = xf - mu
    var = jnp.mean(xc * xc, axis=-1, keepdims=True)
    y = xc * lax.rsqrt(var + EPS)
    return (y * g.astype(jnp.float32) + b.astype(jnp.float32)).astype(x.dtype)


def _rope(t, positions):
    half = HEAD_DIM // 2
    inv_freq = jnp.exp(-math.log(ROPE_THETA) * jnp.arange(half, dtype=jnp.float32) / half)
    ang = positions.astype(jnp.float32)[..., None] * inv_freq
    c = jnp.cos(ang)[:, :, None, :]
    s = jnp.sin(ang)[:, :, None, :]
    tf = t.astype(jnp.float32)
    t1, t2 = tf[..., :half], tf[..., half:]
    return jnp.concatenate([t1 * c - t2 * s, t2 * c + t1 * s], axis=-1).astype(t.dtype)


def _dilated_branch(q, k, v, window, dilation):
    B, S, H, D = q.shape
    r = dilation
    L = S // r
    n_back = window // r
    Lp = ((L + BLK - 1) // BLK) * BLK
    nb = Lp // BLK

    def split(t):
        t = t.reshape(B, L, r, H, D).transpose(0, 2, 1, 3, 4).reshape(B * r, L, H, D)
        return jnp.pad(t, ((0, 0), (0, Lp - L), (0, 0), (0, 0)))

    qs = split(q).reshape(B * r, nb, BLK, H, D)
    kb = split(k).reshape(B * r, nb, BLK, H, D)
    vb = split(v).reshape(B * r, nb, BLK, H, D)
    k_prev = jnp.pad(kb, ((0, 0), (1, 0), (0, 0), (0, 0), (0, 0)))[:, :-1]
    v_prev = jnp.pad(vb, ((0, 0), (1, 0), (0, 0), (0, 0), (0, 0)))[:, :-1]
    kk = jnp.concatenate([k_prev, kb], axis=2)
    vv = jnp.concatenate([v_prev, vb], axis=2)

    scores = jnp.einsum('nbqhd,nbkhd->nbhqk', qs, kk).astype(jnp.float32) * (HEAD_DIM ** -0.5)
    dist = (jnp.arange(BLK)[:, None] + BLK) - jnp.arange(2 * BLK)[None, :]
    band = (dist >= 0) & (dist <= n_back)
    key_idx = jnp.arange(nb)[:, None] * BLK - BLK + jnp.arange(2 * BLK)[None, :]
    valid = band[None, :, :] & (key_idx >= 0)[:, None, :]
    scores = jnp.where(valid[None, :, None], scores, NEG)
    m = jnp.max(scores, axis=-1, keepdims=True)
    e = jnp.exp(scores - m)
    den = jnp.sum(e, axis=-1, keepdims=True)
    out = jnp.einsum('nbhqk,nbkhd->nbqhd', (e / den).astype(v.dtype), vv)
    lse = (m + jnp.log(den))[..., 0]
    lse = lse.transpose(0, 1, 3, 2).reshape(B * r, Lp, H)[:, :L]
    out = out.reshape(B * r, Lp, H, D)[:, :L]
    out = out.reshape(B, r, L, H, D).transpose(0, 2, 1, 3, 4).reshape(B, S, H, D)
    lse = lse.reshape(B, r, L, H).transpose(0, 2, 1, 3).reshape(B, S, H)
    return out, lse


def _dilated_attention(q, k, v):
    outs, lses = [], []
    for window, dilation in DILATIONS:
        o, l = _dilated_branch(q, k, v, window, dilation)
        outs.append(o)
        lses.append(l)
    w = jax.nn.softmax(jnp.stack(lses, axis=0), axis=0)
    o = jnp.sum(w[..., None] * jnp.stack(outs, axis=0).astype(jnp.float32), axis=0)
    return o.astype(q.dtype)


def _gmlp_spatial(u, v, w_s, b_s, ln_g, ln_b):
    B, S, _ = u.shape
    u = jax.nn.gelu(u)
    v = _layernorm(jax.nn.gelu(v), ln_g, ln_b)
    vr = v.reshape(B, S // CHUNK, CHUNK, GMLP_GROUPS, GMLP_GROUP_DIM)
    ws = w_s * jnp.tril(jnp.ones((CHUNK, CHUNK), dtype=w_s.dtype))[None]
    sv = jnp.einsum('gts,bcsgd->bctgd', ws, vr) + b_s.T[None, None, :, :, None]
    return u * sv.reshape(B, S, GMLP_WIDTH)


def setup_inputs(seed: int = 0) -> dict:
    key = jax.random.key(seed)
    ks = jax.random.split(key, 16)
    f32 = jnp.float32
    nrm = lambda k, shape, scale: jax.random.normal(k, shape, f32) * scale
    x = jax.random.normal(ks[0], (BATCH, SEQ, D_MODEL), f32)
    p = jax.random.normal(ks[1], (DEPTH, BATCH, SEQ, PLE_DIM), f32)
    positions = jnp.broadcast_to(jnp.arange(SEQ, dtype=jnp.int32)[None, :], (BATCH, SEQ))
    return {
        "x": x,
        "p": p,
        "positions": positions,
        "g_in": 1.0 + nrm(ks[2], (DEPTH, D_MODEL), 0.01),
        "w_in": nrm(ks[3], (DEPTH, D_MODEL, IN_COLS), D_MODEL ** -0.5),
        "q_norm": 1.0 + nrm(ks[4], (DEPTH, HEAD_DIM), 0.01),
        "k_norm": 1.0 + nrm(ks[5], (DEPTH, HEAD_DIM), 0.01),
        "w_spatial": nrm(ks[6], (DEPTH, GMLP_GROUPS, CHUNK, CHUNK), 0.5 * CHUNK ** -0.5),
        "b_spatial": 1.0 + nrm(ks[7], (DEPTH, GMLP_GROUPS, CHUNK), 0.1),
        "ln_v_g": 1.0 + nrm(ks[8], (DEPTH, GMLP_WIDTH), 0.01),
        "ln_v_b": nrm(ks[9], (DEPTH, GMLP_WIDTH), 0.01),
        "w_out": nrm(ks[10], (DEPTH, MIX_WIDTH, D_MODEL), MIX_WIDTH ** -0.5),
        "g_ple": 1.0 + nrm(ks[11], (DEPTH, D_MODEL), 0.01),
        "w_ple_gate": nrm(ks[12], (DEPTH, D_MODEL, D_MODEL), D_MODEL ** -0.5),
        "b_ple_gate": nrm(ks[13], (DEPTH, D_MODEL), 0.01),
        "w_ple_proj": nrm(ks[14], (DEPTH, PLE_DIM, D_MODEL), PLE_DIM ** -0.5),
    }


def reference(x, p, positions, g_in, w_in, q_norm, k_norm, w_spatial, b_spatial,
              ln_v_g, ln_v_b, w_out, g_ple, w_ple_gate, b_ple_gate, w_ple_proj):
    B, S, _ = x.shape
    h = x
    for i in range(DEPTH):
        a = _rmsnorm(h, g_in[i])
        z = jnp.einsum('bsd,de->bse', a, w_in[i])
        q, k, v, gate_b, u_a, v_a, gate_a = jnp.split(
            z, np.cumsum([ATTN_WIDTH] * 4 + [GMLP_WIDTH] * 2).tolist(), axis=-1)
        q = _rope(_rmsnorm(q.reshape(B, S, N_HEADS, HEAD_DIM), q_norm[i]), positions)
        k = _rope(_rmsnorm(k.reshape(B, S, N_HEADS, HEAD_DIM), k_norm[i]), positions)
        v = v.reshape(B, S, N_HEADS, HEAD_DIM)
        o_b = _dilated_attention(q, k, v).reshape(B, S, ATTN_WIDTH) * jax.nn.silu(gate_b)
        o_a = _gmlp_spatial(u_a, v_a, w_spatial[i], b_spatial[i], ln_v_g[i], ln_v_b[i]) * jax.nn.silu(gate_a)
        mixed = jnp.concatenate([o_b, o_a], axis=-1)
        h = h + jnp.einsum('bse,ed->bsd', mixed, w_out[i])
        gate = jax.nn.sigmoid(jnp.einsum('bsd,de->bse', _rmsnorm(h, g_ple[i]), w_ple_gate[i]) + b_ple_gate[i])
        h = h + gate * jnp.einsum('bsp,pd->bsd', p[i], w_ple_proj[i])
    return h
```

```python
import math
from contextlib import ExitStack

import numpy as np
import ml_dtypes

import concourse.bass as bass
import concourse.mybir as mybir
from concourse.bass_utils import run_bass_kernel_spmd

F32 = mybir.dt.float32
BF = mybir.dt.bfloat16
I32 = mybir.dt.int32
AF = mybir.ActivationFunctionType
ALU = mybir.AluOpType
AX = mybir.AxisListType

ENGS = ("pe", "act", "dve", "pool", "sp")
EPS = 1e-6
NTOK = 4096
NHALO = 2048
NALL = NTOK + NHALO
PI = math.pi


class Buf:
    __slots__ = ("w", "r")

    def __init__(self):
        self.w = None
        self.r = []


class T:
    def __init__(self, t):
        self.t = t
        self.b = Buf()

    def __getitem__(self, k):
        return self.t[k]


class TV(T):
    def __init__(self, base, fn):
        self.b = base.b
        self.fn = fn

    def __getitem__(self, k):
        return self.fn()[k]


class Sched:
    def __init__(self, nc):
        self.nc = nc
        self.ops = {e: [] for e in ENGS}
        self.cnt = {}
        self.sems = {}
        self.seen = {e: {} for e in ENGS}
        self.dma_free = []
        self.gtarget = {}

    def open(self, stack, n_dma_sems):
        for e in ENGS:
            self.sems[e] = stack.enter_context(self.nc.semaphore("s_" + e))
            self.cnt[e] = 0
        for i in range(n_dma_sems):
            nm = "d%d" % i
            self.sems[nm] = stack.enter_context(self.nc.semaphore("s_" + nm))
            self.cnt[nm] = 0
            self.dma_free.append(nm)

    def new_dma_sem(self):
        return self.dma_free.pop(0)

    def _deps(self, eng, reads, writes):
        deps = {}
        for b in reads:
            if b.w is not None:
                s, v = b.w
                deps[s] = max(deps.get(s, 0), v)
        for b in writes:
            if b.w is not None:
                s, v = b.w
                deps[s] = max(deps.get(s, 0), v)
            for (s, v) in b.r:
                deps[s] = max(deps.get(s, 0), v)
        waits = []
        for s, v in deps.items():
            if self.seen[eng].get(s, 0) >= v:
                continue
            self.seen[eng][s] = v
            waits.append((s, v))
        return waits

    def _reg(self, tok, reads, writes):
        for b in reads:
            b.r.append(tok)
            if len(b.r) > 64:
                best = {}
                for s, v in b.r:
                    best[s] = max(best.get(s, 0), v)
                b.r = list(best.items())
        for b in writes:
            b.w = tok
            b.r = []

    def op(self, eng, fn, reads=(), writes=(), sig=True):
        reads = [x.b if isinstance(x, T) else x for x in reads]
        writes = [x.b if isinstance(x, T) else x for x in writes]
        waits = self._deps(eng, reads, writes)
        if sig:
            self.cnt[eng] += 1
            self._reg((eng, self.cnt[eng]), reads, writes)
        self.ops[eng].append((waits, fn, (eng, 1) if sig else None))

    def group(self, sem, n):
        self.gtarget[sem] = self.cnt[sem] + 16 * n

    def dma(self, sem, fn, reads=(), writes=(), eng="sp"):
        reads = [x.b if isinstance(x, T) else x for x in reads]
        writes = [x.b if isinstance(x, T) else x for x in writes]
        tgt = self.gtarget.get(sem, 0)
        saved = self.seen[eng].get(sem, 0)
        if tgt > self.cnt[sem]:
            self.seen[eng][sem] = max(saved, tgt)
        waits = self._deps(eng, reads, writes)
        self.seen[eng][sem] = max(saved, max([v for (s_, v) in waits if s_ == sem] + [saved]))
        self.cnt[sem] += 16
        tokv = tgt if tgt >= self.cnt[sem] else self.cnt[sem]
        self._reg((sem, tokv), reads, writes)
        self.ops[eng].append((waits, fn, (sem, 16)))

    def wait_all(self, eng):
        waits = []
        for s, v in self.cnt.items():
            if s == eng:
                continue
            if v > 0 and self.seen[eng].get(s, 0) < v:
                self.seen[eng][s] = v
                waits.append((s, v))
        self.ops[eng].append((waits, None, None))

    def barrier(self):
        for e in ENGS:
            self.wait_all(e)

    def emit(self, block):
        def run(ename):
            ops = self.ops[ename]

            def body(h):
                for waits, fn, inc in ops:
                    for s, v in waits:
                        h.wait_ge(self.sems[s], v)
                    if fn is None:
                        continue
                    ins = fn(h)
                    if inc is not None:
                        ins.then_inc(self.sems[inc[0]], inc[1])
            return body

        block.tensor(run("pe"))
        block.scalar(run("act"))
        block.vector(run("dve"))
        block.gpsimd(run("pool"))
        block.sync(run("sp"))
        self.ops = {e: [] for e in ENGS}


def run_chains_g(makers, W):
    pending = list(makers)
    active = []
    free = list(range(W))
    while pending or active:
        while pending and free:
            slot = free.pop(0)
            active.append([slot, pending.pop(0)(slot), 0])
        for a in list(active):
            a[1][a[2]]()
            a[2] += 1
            if a[2] == len(a[1]):
                active.remove(a)
                free.append(a[0])


def _const_tables():
    ident = np.eye(128, dtype=np.float32).astype(ml_dtypes.bfloat16)
    n = np.arange(128)
    pos16 = n
    pos4 = n
    pos1 = 4 * (n % 32) + n // 32
    masks = np.zeros((128, 6, 128), np.float32)
    for br, pos in enumerate((pos16, pos4, pos1)):
        pk = n[:, None]
        pq = pos[None, :]
        masks[:, 2 * br + 0, :] = (pk >= pq)
        masks[:, 2 * br + 1, :] = (pk <= pq)
    masks = masks.astype(ml_dtypes.bfloat16)
    half = 32
    invf = np.exp(-math.log(10000.0) * np.arange(half, dtype=np.float32) / half).astype(np.float32)
    invf = np.broadcast_to(invf[None, :], (128, 32)).copy()
    tril = (n[:, None] <= n[None, :]).astype(np.float32)
    return ident, masks, invf, tril


def build_nc(stop=None, dbg=False, skipB=False):
    nc = bass.Bass("TRN2", target_bir_lowering=False)
    sk = "ExternalOutput"

    def din(name, shape, dt=F32):
        return nc.dram_tensor(name, shape, dt, kind="ExternalInput").ap()

    xh = din("xh", [NALL, 1024])
    pown = din("pown", [NTOK, 256])
    pos_t = din("pos_t", [128, 48], I32)
    flag_d = din("flag", [128, 1])
    w_in = din("w_in", [1024, 3584])
    w_out = din("w_out", [1024, 1024])
    w_gate = din("w_gate", [1024, 1024])
    w_pp = din("w_pp", [256, 1024])
    gin_d = din("gin", [128, 8])
    gple_d = din("gple", [128, 8])
    qn_d = din("qn_b", [128, 64])
    kn_d = din("kn_b", [128, 64])
    lng_d = din("lng", [128, 4])
    lnb_d = din("lnb", [128, 4])
    bsb_d = din("bsb", [128, 512])
    wsT_d = din("wsT", [128, 512])
    bgate_d = din("bgate", [1, 1024])
    ident_d = din("ident", [128, 128], BF)
    masks_d = din("masks", [128, 768], BF)
    invf_d = din("invf", [128, 32])
    tril_d = din("tril", [128, 128])
    out = nc.dram_tensor("out", [NTOK, 1024], F32, kind="ExternalOutput").ap()

    kT_d = nc.dram_tensor("kT_d", [4, 128, NALL], BF, kind=sk).ap()
    vT_d = nc.dram_tensor("vT_d", [4, 128, NALL], BF, kind=sk).ap()
    qT_d = nc.dram_tensor("qT_d", [4, 128, NTOK], BF, kind=sk).ap()
    gbT_d = nc.dram_tensor("gbT_d", [4, 128, NTOK], BF, kind=sk).ap()

    with ExitStack() as G:
        S = Sched(nc)
        S.open(G, 64)

        def sb(stack, name, shape, dt):
            return T(stack.enter_context(nc.sbuf_tensor("sb_" + name, shape, dt)))

        def ps(stack, name, shape, dt):
            return T(stack.enter_context(nc.psum_tensor("ps_" + name, shape, dt)))

        ident = sb(G, "ident", [128, 128], BF)
        masks = sb(G, "masks", [128, 6, 128], BF)
        flag = sb(G, "flag", [128, 1], F32)
        gmT = sb(G, "gmT", [128, 4, NTOK], BF)
        ones_bf = sb(G, "ones_bf", [128, 128], BF)
        dsem_c = S.new_dma_sem()

        def ld(dst, dst_ap, src_ap, sem=None):
            S.dma(sem or S.new_dma_sem(), lambda e: e.dma_start(out=dst_ap, in_=src_ap), writes=[dst])

        ld(ident, ident[:, :], ident_d[:, :])
        ld(masks, masks[:, :, :], masks_d[:, :].rearrange("p (a b) -> p a b", a=6))
        ld(flag, flag[:, :], flag_d[:, :])
        S.op("pool", lambda e: e.memset(ones_bf[:, :], 1.0), writes=[ones_bf])
        mhalf = sb(G, "mhalf", [128, 8], F32)
        S.op("pool", lambda e: e.memset(mhalf[:, :], -0.5), writes=[mhalf])

        with ExitStack() as A:
            Wb = sb(A, "Wb", [128, 8, 3584], BF)
            gin = sb(A, "gin", [128, 8], F32)
            qnb = sb(A, "qnb", [128, 64], F32)
            knb = sb(A, "knb", [128, 64], F32)
            gq1 = sb(A, "gq1", [128, 64], F32)
            gq2 = sb(A, "gq2", [128, 64], F32)
            gk1 = sb(A, "gk1", [128, 64], F32)
            gk2 = sb(A, "gk2", [128, 64], F32)
            lng = sb(A, "lng", [128, 4], F32)
            lnb = sb(A, "lnb", [128, 4], F32)
            wsb = sb(A, "wsb", [128, 4, 128], BF)
            tril = sb(A, "tril", [128, 128], F32)
            biasT = sb(A, "biasT", [128, 4, 128], F32)
            invf = sb(A, "invf", [128, 32], F32)
            posi = sb(A, "posi", [128, 48], I32)
            posf = sb(A, "posf", [128, 48], F32)

            xt = [sb(A, "xt%d" % i, [128, 1024], F32) for i in range(3)]
            xsq = sb(A, "xsq", [128, 1024], BF)
            xn = [sb(A, "xn%d" % i, [128, 1024], BF) for i in range(2)]
            xnT = [sb(A, "xnT%d" % i, [128, 8, 512], BF) for i in range(2)]
            st1 = [sb(A, "st1_%d" % i, [128, 4], F32) for i in range(3)]
            ang = sb(A, "ang", [128, 4, 32], F32)
            ni = sb(A, "ni", [128, 4, 32], I32)
            nf = sb(A, "nf", [128, 4, 32], F32)
            rr_ = sb(A, "rr_", [128, 4, 32], F32)
            arg = sb(A, "arg", [128, 2, 4, 32], F32)
            cs = sb(A, "cs", [128, 2, 4, 32], F32)
            Gq = sb(A, "Gq", [128, 2, 4, 64], F32)
            Gk = sb(A, "Gk", [128, 2, 4, 64], F32)
            NSET = 6
            sets = []
            for i in range(NSET):
                sets.append(dict(f0=sb(A, "f0_%d" % i, [128, 512], F32), f1=sb(A, "f1_%d" % i, [128, 512], F32),
                                 f2=sb(A, "f2_%d" % i, [128, 512], F32), b0=sb(A, "b0_%d" % i, [128, 512], BF),
                                 s8a=sb(A, "s8a_%d" % i, [128, 8], F32), s8b=sb(A, "s8b_%d" % i, [128, 8], F32),
                                 yst=sb(A, "yst_%d" % i, [128, 4], F32)))
            bsb = TV(sets[0]["f0"], lambda: sets[0]["f0"][:, :].rearrange("p (g t) -> p g t", g=4))
            wsT = TV(sets[0]["f1"], lambda: sets[0]["f1"][:, :].rearrange("p (g t) -> p g t", g=4))
            uT = sb(A, "uT", [128, 4, 512], BF)
            gaT = sb(A, "gaT", [128, 4, 512], BF)
            w1 = uT
            wst = [TV(uT, lambda: uT[:, :, :].rearrange("p a l -> p (a l)").bitcast(F32)[:, 0:896]),
                   TV(gaT, lambda: gaT[:, :, :].rearrange("p a l -> p (a l)").bitcast(F32)[:, 0:896])]
            kst = [sb(A, "kst0", [128, 4, 512], BF)] * 2
            qst = [sb(A, "qst0", [128, 4, 512], BF)] * 2
            vst = [sb(A, "vst0", [128, 4, 512], BF)] * 2
            gst = [sb(A, "gst0", [128, 4, 512], BF)] * 2

            tpx = [ps(A, "tpx%d" % i, [128, 8, 128], BF) for i in range(2)]
            pm = [ps(A, "pm%d" % i, [128, 512], F32) for i in range(6)]
            pmain = pm
            paux = pm
            tpq = [TV(pm[i], (lambda i=i: pm[i][:, :].bitcast(BF).rearrange("p (a t) -> p a t", a=8))) for i in range(6)]

            for (dst, src) in ((gin, gin_d), (qnb, qn_d), (knb, kn_d), (lng, lng_d), (lnb, lnb_d),
                               (tril, tril_d), (invf, invf_d), (posi, pos_t)):
                ld(dst, dst[:, :], src[:, :])
            ld(bsb, bsb[:, :, :], bsb_d[:, :].rearrange("p (g t) -> p g t", g=4))
            ld(wsT, wsT[:, :, :], wsT_d[:, :].rearrange("p (g t) -> p g t", g=4))
            S.op("dve", lambda e: e.tensor_copy(posf[:, :], posi[:, :]), reads=[posi], writes=[posf])
            for (src, g1, g2) in ((qnb, gq1, gq2), (knb, gk1, gk2)):
                S.op("dve", lambda e, src=src, g1=g1: e.tensor_scalar(g1[:, :], src[:, :], 1.0, None, ALU.mult),
                     reads=[src], writes=[g1])
                S.op("dve", lambda e, src=src, g2=g2: e.tensor_scalar(g2[:, 0:32], src[:, 32:64], -1.0, None, ALU.mult),
                     reads=[src], writes=[g2])
                S.op("dve", lambda e, src=src, g2=g2: e.tensor_scalar(g2[:, 32:64], src[:, 0:32], 1.0, None, ALU.mult),
                     reads=[src, g2], writes=[g2])
            S.op("dve", lambda e: e.tensor_tensor(out=wsb[:, :, :], in0=wsT[:, :, :],
                                                  in1=tril[:, :].unsqueeze(1).broadcast_to([128, 4, 128]), op=ALU.mult),
                 reads=[wsT, tril], writes=[wsb])
            for g in range(4):
                S.op("pe", lambda e, g=g: e.matmul(pm[0][:, g * 128:(g + 1) * 128], lhsT=ones_bf[:, :], rhs=wsb[:, g, :],
                                                   start=True, stop=True, skip_group_check=True),
                     reads=[ones_bf, wsb], writes=[pm[0]])
            for g in range(4):
                S.op("dve", lambda e, g=g: e.scalar_tensor_tensor(out=biasT[:, g, :], in0=pm[0][:, g * 128:(g + 1) * 128],
                                                                  scalar=lnb[:, g:g + 1], in1=bsb[:, g, :],
                                                                  op0=ALU.mult, op1=ALU.add),
                     reads=[pm[0], lnb, bsb, biasT], writes=[biasT])

            wsem = [S.new_dma_sem() for _ in range(2)]
            cnt = 0
            for half in range(4):
                for kc in range(8):
                    stg = wst[cnt % 2]
                    S.dma(wsem[cnt % 2], lambda e, stg=stg, kc=kc, half=half: e.dma_start(
                        out=stg[:, :], in_=w_in[kc * 128:(kc + 1) * 128, half * 896:(half + 1) * 896]), writes=[stg])
                    if cnt % 2 == 0:
                        S.op("dve", lambda e, stg=stg, kc=kc, half=half: e.tensor_scalar(
                            Wb[:, kc, half * 896:(half + 1) * 896], stg[:, :], gin[:, kc:kc + 1], None, ALU.mult),
                            reads=[stg, gin], writes=[Wb])
                    else:
                        S.op("act", lambda e, stg=stg, kc=kc, half=half: e.activation(
                            out=Wb[:, kc, half * 896:(half + 1) * 896], in_=stg[:, :], func=AF.Copy,
                            scale=gin[:, kc:kc + 1]), reads=[stg, gin], writes=[Wb])
                    cnt += 1

            xsem = [S.new_dma_sem() for _ in range(3)]
            stsem = [S.new_dma_sem() for _ in range(2)]
            pmi = [0]

            def next_pm():
                b = pm[pmi[0] % 6]
                pmi[0] += 1
                return b

            def load_x(ti):
                d = xt[ti % 3]
                S.dma(xsem[ti % 3], lambda e, d=d, ti=ti: e.dma_start(out=d[:, :], in_=xh[ti * 128:(ti + 1) * 128, :]),
                      writes=[d])

            def rr(x):
                return x[:, :].rearrange("p (h d) -> p h d", h=8)

            def qk_chain(kind, B, tl, slot):
                sc = sets[slot]
                f0, f1, f2, b0, s8a, s8b = sc["f0"], sc["f1"], sc["f2"], sc["b0"], sc["s8a"], sc["s8b"]
                xb = xnT[B % 2]
                col0 = 512 if kind == "k" else 0
                G_ = Gk if kind == "k" else Gq
                dst = (kst if kind == "k" else qst)[B % 2]
                tpb = tpq[slot]
                permute = kind == "q"
                st = {}

                def t0():
                    pb = st["pb"] = pmain[slot]
                    for kc in range(8):
                        S.op("pe", lambda e, kc=kc: e.matmul(pb[:, :], lhsT=xb[:, kc, tl * 128:(tl + 1) * 128], rhs=Wb[:, kc, col0:col0 + 512],
                                                             start=(kc == 0), stop=(kc == 7), skip_group_check=True),
                             reads=[Wb, xb], writes=[pb], sig=(kc == 7))

                def t1_():
                    pb = st["pb"]
                    S.op("act", lambda e: e.activation(out=f0[:, :], in_=pb[:, :], func=AF.Square), reads=[pb], writes=[f0])

                def t2_():
                    S.op("dve", lambda e: e.reduce_sum(out=s8a[:, :], in_=rr(f0), axis=AX.X), reads=[f0], writes=[s8a])
                    S.op("dve", lambda e: e.tensor_scalar(s8b[:, :], s8a[:, :], 1.0 / 64, EPS, ALU.mult, ALU.add), reads=[s8a], writes=[s8b])

                def t3_():
                    S.op("pool", lambda e: e.tensor_tensor(out=s8b[:, :], in0=s8b[:, :], in1=mhalf[:, :], op=ALU.pow),
                         reads=[s8b, mhalf], writes=[s8b])

                def t4_():
                    pb = st["pb"]
                    S.op("dve", lambda e: e.tensor_tensor(out=rr(f1), in0=rr(pb), in1=s8b[:, :].unsqueeze(2).broadcast_to([128, 8, 64]), op=ALU.mult),
                         reads=[pb, s8b], writes=[f1])

                def t5_():
                    ga = G_[:, 0, tl, :].unsqueeze(1).broadcast_to([128, 8, 64])
                    gb0 = G_[:, 1, tl, 0:32].unsqueeze(1).broadcast_to([128, 8, 32])
                    gb1 = G_[:, 1, tl, 32:64].unsqueeze(1).broadcast_to([128, 8, 32])
                    S.op("dve", lambda e: e.tensor_tensor(out=rr(f0), in0=rr(f1), in1=ga, op=ALU.mult), reads=[f1, G_], writes=[f0])
                    S.op("pool", lambda e: e.tensor_tensor(out=rr(f2)[:, :, 0:32], in0=rr(f1)[:, :, 32:64], in1=gb0, op=ALU.mult),
                         reads=[f1, G_], writes=[f2])
                    S.op("pool", lambda e: e.tensor_tensor(out=rr(f2)[:, :, 32:64], in0=rr(f1)[:, :, 0:32], in1=gb1, op=ALU.mult),
                         reads=[f1, G_, f2], writes=[f2])

                def t6_():
                    S.op("dve", lambda e: e.tensor_tensor(out=b0[:, :], in0=f0[:, :], in1=f2[:, :], op=ALU.add), reads=[f0, f2], writes=[b0])

                def t7_():
                    for pr in range(4):
                        S.op("pe", lambda e, pr=pr: e.transpose(tpb[:, pr, :], b0[:, pr * 128:(pr + 1) * 128], ident[:, :]),
                             reads=[b0, ident], writes=[tpb], sig=(pr == 3))

                def t8_():
                    if permute:
                        src = tpb[:, 0:4, :].rearrange("p a (n c) -> p a c n", c=4)
                        dsta = dst[:, :, :].rearrange("p a (c t n) -> p a c t n", c=4, t=4)[:, :, :, tl, :]
                    else:
                        src = tpb[:, 0:4, :]
                        dsta = dst[:, :, tl * 128:(tl + 1) * 128]
                    S.op("act", lambda e: e.activation(out=dsta, in_=src, func=AF.Copy), reads=[tpb], writes=[dst])

                return [t0, t1_, t2_, t3_, t4_, t5_, t6_, t7_, t8_]

            def va_chain(B, tl, slot):
                sc = sets[slot]
                yv, svv, y_n, yst = sc["f1"], sc["f2"], sc["b0"], sc["yst"]
                xb = xnT[B % 2]
                st = {}

                def t0():
                    pb = st["pb"] = pmain[slot]
                    for kc in range(8):
                        S.op("pe", lambda e, kc=kc: e.matmul(pb[:, :], lhsT=xb[:, kc, tl * 128:(tl + 1) * 128], rhs=Wb[:, kc, 2560:3072],
                                                             start=(kc == 0), stop=(kc == 7), skip_group_check=True),
                             reads=[Wb, xb], writes=[pb], sig=(kc == 7))
                    S.op("pool", lambda e: e.memset(yst[:, :], 0.0), writes=[yst])

                def t1_():
                    pb = st["pb"]
                    S.op("act", lambda e: e.activation(out=yv[:, :], in_=pb[:, :], func=AF.Gelu_apprx_tanh, accum_out=yst[:, 0:1]),
                         reads=[pb, yst], writes=[yv, yst])

                def t2_():
                    S.op("dve", lambda e: e.tensor_scalar(yst[:, 1:2], yst[:, 0:1], -1.0 / 512, None, ALU.mult), reads=[yst], writes=[yst])

                def t3_():
                    S.op("act", lambda e: e.activation(out=xsq[:, 0:512], in_=yv[:, :], func=AF.Square, bias=yst[:, 1:2], accum_out=yst[:, 2:3]),
                         reads=[yv, yst], writes=[xsq, yst])

                def t4_():
                    S.op("dve", lambda e: e.tensor_scalar(yst[:, 3:4], yst[:, 2:3], 1.0 / 512, EPS, ALU.mult, ALU.add), reads=[yst], writes=[yst])

                def t5_():
                    S.op("pool", lambda e: e.tensor_tensor(out=yst[:, 3:4], in0=yst[:, 3:4], in1=mhalf[:, 0:1], op=ALU.pow),
                         reads=[yst, mhalf], writes=[yst])

                def t6_():
                    S.op("dve", lambda e: e.tensor_scalar(y_n[:, :], yv[:, :], yst[:, 1:2], yst[:, 3:4], ALU.add, ALU.mult),
                         reads=[yv, yst], writes=[y_n])

                def t7_():
                    pb2 = st["pb2"] = paux[slot]
                    for g in range(4):
                        S.op("pe", lambda e, g=g: e.matmul(pb2[:, g * 128:(g + 1) * 128], lhsT=y_n[:, g * 128:(g + 1) * 128], rhs=wsb[:, g, :],
                                                           start=True, stop=True, skip_group_check=True),
                             reads=[y_n, wsb], writes=[pb2], sig=(g == 3))

                def t8_():
                    pb2 = st["pb2"]
                    sv3 = svv[:, :].rearrange("p (g t) -> p g t", g=4)
                    for g in range(4):
                        S.op("dve", lambda e, g=g: e.scalar_tensor_tensor(out=sv3[:, g, :], in0=pb2[:, g * 128:(g + 1) * 128],
                                                                          scalar=lng[:, g:g + 1], in1=biasT[:, g, :], op0=ALU.mult, op1=ALU.add),
                             reads=[pb2, lng, biasT, svv], writes=[svv])

                def t9_():
                    own0 = (B - 4) * 512
                    dsta = gmT[:, :, own0:own0 + 512].rearrange("p g (c t n) -> p g c t n", c=4, t=4)[:, :, :, tl, :]
                    in0 = svv[:, :].rearrange("p (g n c) -> p g c n", g=4, c=4)
                    in1 = w1[:, :, tl * 128:(tl + 1) * 128].rearrange("p g (n c) -> p g c n", c=4)
                    S.op("dve", lambda e: e.tensor_tensor(out=dsta, in0=in0, in1=in1, op=ALU.mult), reads=[svv, w1], writes=[gmT])

                return [t0, t1_, t2_, t3_, t4_, t5_, t6_, t7_, t8_, t9_]

            def stage1_chain(ti, slot):
                B, tl = ti // 4, ti % 4
                x_, s_, x_n, tp, xb = xt[ti % 3], st1[ti % 3], xn[ti % 2], tpx[ti % 2], xnT[B % 2]

                def t0():
                    S.op("pool", lambda e: e.memset(s_[:, :], 0.0), writes=[s_])
                    S.op("act", lambda e: e.activation(out=xsq[:, :], in_=x_[:, :], func=AF.Square, accum_out=s_[:, 0:1]),
                         reads=[x_, s_], writes=[xsq, s_])

                def t1_():
                    S.op("dve", lambda e: e.tensor_scalar(s_[:, 1:2], s_[:, 0:1], 1.0 / 1024, EPS, ALU.mult, ALU.add), reads=[s_], writes=[s_])

                def t2_():
                    S.op("pool", lambda e: e.tensor_tensor(out=s_[:, 2:3], in0=s_[:, 1:2], in1=mhalf[:, 0:1], op=ALU.pow),
                         reads=[s_, mhalf], writes=[s_])

                def t3_():
                    S.op("act", lambda e: e.activation(out=x_n[:, :], in_=x_[:, :], func=AF.Copy, scale=s_[:, 2:3]), reads=[x_, s_], writes=[x_n])
                    if ti + 3 < 48:
                        load_x(ti + 3)

                def t4_():
                    for kc in range(8):
                        S.op("pe", lambda e, kc=kc: e.transpose(tp[:, kc, :], x_n[:, kc * 128:(kc + 1) * 128], ident[:, :]),
                             reads=[x_n, ident], writes=[tp], sig=(kc == 7))

                def t5_():
                    S.op("act", lambda e: e.activation(out=xb[:, :, tl * 128:(tl + 1) * 128], in_=tp[:, :, :], func=AF.Copy), reads=[tp], writes=[xb])

                return [t0, t1_, t2_, t3_, t4_, t5_]

            def run_chains(makers, W):
                pending = list(makers)
                active = []
                free = list(range(W))
                while pending or active:
                    while pending and free:
                        slot = free.pop(0)
                        active.append([slot, pending.pop(0)(slot), 0])
                    for a in list(active):
                        a[1][a[2]]()
                        a[2] += 1
                        if a[2] == len(a[1]):
                            active.remove(a)
                            free.append(a[0])

            def rope_tables(B):
                halo = B < 4
                C1 = 6.28125
                C2 = 2 * PI - 6.28125
                S.op("dve", lambda e: e.tensor_tensor(out=ang[:, :, :], in0=invf[:, :].unsqueeze(1).broadcast_to([128, 4, 32]),
                                                      in1=posf[:, 4 * B:4 * B + 4].unsqueeze(2).broadcast_to([128, 4, 32]), op=ALU.mult),
                     reads=[invf, posf, ang], writes=[ang])
                S.op("dve", lambda e: e.tensor_scalar(ni[:, :, :], ang[:, :, :], 1.0 / (2 * PI), None, ALU.mult), reads=[ang, ni], writes=[ni])
                S.op("dve", lambda e: e.tensor_copy(nf[:, :, :], ni[:, :, :]), reads=[ni, nf], writes=[nf])
                S.op("dve", lambda e: e.scalar_tensor_tensor(out=rr_[:, :, :], in0=nf[:, :, :], scalar=-C1, in1=ang[:, :, :],
                                                             op0=ALU.mult, op1=ALU.add), reads=[nf, ang, rr_], writes=[rr_])
                S.op("dve", lambda e: e.scalar_tensor_tensor(out=rr_[:, :, :], in0=nf[:, :, :], scalar=-C2, in1=rr_[:, :, :],
                                                             op0=ALU.mult, op1=ALU.add), reads=[nf, rr_], writes=[rr_])
                S.op("dve", lambda e: e.tensor_scalar(nf[:, :, :], rr_[:, :, :], PI, 2 * PI, ALU.is_gt, ALU.mult), reads=[rr_, nf], writes=[nf])
                S.op("dve", lambda e: e.tensor_tensor(out=arg[:, 0, :, :], in0=rr_[:, :, :], in1=nf[:, :, :], op=ALU.subtract),
                     reads=[rr_, nf, arg], writes=[arg])
                S.op("dve", lambda e: e.tensor_scalar(rr_[:, :, :], arg[:, 0, :, :], 0.5 * PI, None, ALU.add), reads=[arg, rr_], writes=[rr_])
                S.op("dve", lambda e: e.tensor_scalar(nf[:, :, :], rr_[:, :, :], PI, 2 * PI, ALU.is_gt, ALU.mult), reads=[rr_, nf], writes=[nf])
                S.op("dve", lambda e: e.tensor_tensor(out=arg[:, 1, :, :], in0=rr_[:, :, :], in1=nf[:, :, :], op=ALU.subtract),
                     reads=[rr_, nf, arg], writes=[arg])
                S.op("act", lambda e: e.activation(out=cs[:, :, :, :], in_=arg[:, :, :, :], func=AF.Sin), reads=[arg, cs], writes=[cs])
                for (G_, g1, g2, need) in ((Gq, gq1, gq2, not halo), (Gk, gk1, gk2, True)):
                    if not need:
                        continue
                    cos2 = cs[:, 1, :, :].unsqueeze(2).broadcast_to([128, 4, 2, 32])
                    sin2 = cs[:, 0, :, :].unsqueeze(2).broadcast_to([128, 4, 2, 32])
                    S.op("dve", lambda e, G_=G_, g1=g1, cos2=cos2: e.tensor_tensor(
                        out=G_[:, 0, :, :].rearrange("p t (a d) -> p t a d", a=2), in0=cos2,
                        in1=g1[:, :].rearrange("p (a d) -> p a d", a=2).unsqueeze(1).broadcast_to([128, 4, 2, 32]),
                        op=ALU.mult), reads=[cs, g1, G_], writes=[G_])
                    S.op("dve", lambda e, G_=G_, g2=g2, sin2=sin2: e.tensor_tensor(
                        out=G_[:, 1, :, :].rearrange("p t (a d) -> p t a d", a=2), in0=sin2,
                        in1=g2[:, :].rearrange("p (a d) -> p a d", a=2).unsqueeze(1).broadcast_to([128, 4, 2, 32]),
                        op=ALU.mult), reads=[cs, g2, G_], writes=[G_])

            load_x(0)
            load_x(1)
            load_x(2)
            run_chains([(lambda slot, ti=ti: stage1_chain(ti, slot)) for ti in range(4)], 2)
            for B in range(12):
                halo = B < 4
                xb = xnT[B % 2]
                rope_tables(B)
                vs_ = vst[B % 2]
                gs_ = gst[B % 2]
                ks_ = kst[B % 2]
                qs_ = qst[B % 2]
                fm = [("v", 1024, vs_)]
                if not halo:
                    fm += [("gb", 1536, gs_), ("ga", 3072, gaT), ("u", 2048, uT)]
                for (kind, col0, dstT) in fm:
                    for j in range(4):
                        pb = next_pm()
                        for kc in range(8):
                            S.op("pe", lambda e, pb=pb, kc=kc, xb=xb, c0=col0 + j * 128: e.matmul(
                                pb[:, :], lhsT=Wb[:, kc, c0:c0 + 128], rhs=xb[:, kc, :], start=(kc == 0), stop=(kc == 7),
                                skip_group_check=True), reads=[Wb, xb], writes=[pb], sig=(kc == 7))
                        if kind == "gb":
                            src = pb[:, :].rearrange("p (n c) -> p c n", c=4)
                            dsta = dstT[:, j, :].rearrange("p (c n) -> p c n", c=4)
                        else:
                            src = pb[:, :]
                            dsta = dstT[:, j, :]
                        if kind == "v":
                            S.op("act", lambda e, src=src, dsta=dsta: e.activation(out=dsta, in_=src, func=AF.Copy), reads=[pb], writes=[dstT])
                        else:
                            fn = AF.Gelu_apprx_tanh if kind == "u" else AF.Silu
                            S.op("act", lambda e, src=src, dsta=dsta, fn=fn: e.activation(out=dsta, in_=src, func=fn),
                                 reads=[pb], writes=[dstT])
                if not halo:
                    S.op("pool", lambda e: e.tensor_tensor(out=w1[:, :, :], in0=uT[:, :, :], in1=gaT[:, :, :], op=ALU.mult),
                         reads=[uT, gaT], writes=[w1])

                makers = []
                for tl in range(4):
                    makers.append(lambda slot, tl=tl, B=B: qk_chain("k", B, tl, slot))
                    if not halo:
                        makers.append(lambda slot, tl=tl, B=B: qk_chain("q", B, tl, slot))
                        makers.append(lambda slot, tl=tl, B=B: va_chain(B, tl, slot))
                    if B + 1 < 12:
                        makers.append(lambda slot, ti=4 * (B + 1) + tl: stage1_chain(ti, slot))
                run_chains(makers, 5 if halo else NSET)

                c0 = B * 512
                ssem = stsem[B % 2]
                S.group(ssem, 2 if halo else 4)
                S.dma(ssem, lambda e, ks_=ks_, c0=c0: e.dma_start(out=kT_d[:, :, c0:c0 + 512].rearrange("a p l -> p a l"),
                                                                 in_=ks_[:, :, :]), reads=[ks_])
                S.dma(ssem, lambda e, vs_=vs_, c0=c0: e.dma_start(out=vT_d[:, :, c0:c0 + 512].rearrange("a p l -> p a l"),
                                                                 in_=vs_[:, :, :]), reads=[vs_])
                if not halo:
                    o0 = c0 - NHALO
                    S.dma(ssem, lambda e, qs_=qs_, o0=o0: e.dma_start(out=qT_d[:, :, o0:o0 + 512].rearrange("a p l -> p a l"),
                                                                     in_=qs_[:, :, :]), reads=[qs_])
                    S.dma(ssem, lambda e, gs_=gs_, o0=o0: e.dma_start(out=gbT_d[:, :, o0:o0 + 512].rearrange("a p l -> p a l"),
                                                                     in_=gs_[:, :, :]), reads=[gs_])
            S.barrier()
            with nc.Block() as block:
                S.emit(block)

        if stop == "A":
            return nc
        with ExitStack() as BC:
            atT = sb(BC, "atT", [128, 4, NTOK], BF)
            atT_hi = Buf()
            Wo = sb(BC, "Wo", [128, 8, 1024], BF)
            Wg = sb(BC, "Wg", [128, 8, 1024], BF)
            Wp = sb(BC, "Wp", [128, 2, 1024], BF)
            gple = sb(BC, "gple", [128, 8], F32)
            bg_b = sb(BC, "bg_b", [1, 1024], BF)
            ones1 = sb(BC, "ones1", [1, 128], BF)
            ld(gple, gple[:, :], gple_d[:, :])
            S.op("dve", lambda e: e.memset(ones1[:, :], 1.0), writes=[ones1])

            with ExitStack() as Bs:
                wst2 = [sb(Bs, "wst2_0", [128, 1024], F32)] * 2
                ld(wst2[0], wst2[0][0:1, :], bgate_d[:, :])
                S.op("dve", lambda e: e.tensor_copy(bg_b[:, :], wst2[0][0:1, :]), reads=[wst2[0]], writes=[bg_b])
                w2sem = [S.new_dma_sem() for _ in range(2)]
                jobs = [(w_out, Wo, kc, None) for kc in range(8)] + [(w_gate, Wg, kc, gple) for kc in range(8)] + \
                       [(w_pp, Wp, kc, None) for kc in range(2)]
                jcnt = [0]

                def tail_weight_jobs(n):
                    for _ in range(n):
                        if not jobs:
                            return
                        (src, dstW, kc, gsc) = jobs.pop(0)
                        cnt = jcnt[0]
                        jcnt[0] += 1
                        stg = wst2[cnt % 2]
                        S.dma(w2sem[cnt % 2], lambda e, stg=stg, src=src, kc=kc: e.dma_start(out=stg[:, :], in_=src[kc * 128:(kc + 1) * 128, :]),
                              writes=[stg])
                        if gsc is None:
                            S.op("dve", lambda e, stg=stg, dstW=dstW, kc=kc: e.tensor_copy(dstW[:, kc, :], stg[:, :]),
                                 reads=[stg], writes=[dstW])
                        else:
                            S.op("dve", lambda e, stg=stg, dstW=dstW, kc=kc, gsc=gsc: e.tensor_scalar(
                                dstW[:, kc, :], stg[:, :], gsc[:, kc:kc + 1], None, ALU.mult), reads=[stg, gsc], writes=[dstW])

                qs = [sb(Bs, "qs%d" % i, [128, 2048], BF) for i in range(2)]
                ksb = [sb(Bs, "ksb%d" % i, [128, 4096], BF) for i in range(2)]
                vsb = [sb(Bs, "vsb%d" % i, [128, 4096], BF) for i in range(2)]
                gsb = [sb(Bs, "gsb%d" % i, [128, 2048], BF) for i in range(2)]
                Vc = sb(Bs, "Vc", [128, 69, 192], BF)
                LA = 4
                NST = 4
                LAS = 1
                NPT = 6
                pt = [sb(Bs, "pt%d" % i, [128, 512], BF) for i in range(NPT)]
                rec = sb(Bs, "rec", [128, 2048], F32)
                accm = [ps(Bs, "accm%d" % i, [128, 512], F32) for i in range(2)]
                acc16p = ps(Bs, "acc16p", [128, 512], F32)
                a16 = sb(Bs, "a16", [128, 2048], F32)
                stp = [ps(Bs, "stp%d" % i, [128, 512], F32) for i in range(LA + 1)]
                vtp = [TV(stp[i], (lambda i=i: stp[i][:, :].bitcast(BF).rearrange("p (a t) -> p a t", a=8))) for i in range(LA + 1)]
                bsem = [S.new_dma_sem() for _ in range(2)]

                S.op("pool", lambda e: e.memset(Vc[:, :, 64:128], 1.0), writes=[Vc])

                def chunk(ten, rs, kind, sbk, a, b=0):
                    base = ten[rs, sbk * 2048:(sbk + 1) * 2048]
                    if kind == 0:
                        return base.rearrange("p (k c i x) -> p k c i x", k=4, c=4, x=4)[:, :, a % 4, :, a // 4]
                    if kind == 1:
                        return base[:, 512 * b + 128 * a:512 * b + 128 * a + 128]
                    k, q4 = a // 4, a % 4
                    return base[:, 512 * k:512 * k + 512].rearrange("p (c q n) -> p c q n", c=4, q=4)[:, :, q4, :]

                def kchunk(ten, rs, kind, sbk, a, b=0):
                    if kind == 0:
                        st0 = sbk * 2048 + a
                        return ten[rs, st0:st0 + 16 * 127 + 1:16]
                    if kind == 1:
                        st0 = sbk * 2048 + 512 * b + a
                        return ten[rs, st0:st0 + 4 * 127 + 1:4]
                    st0 = sbk * 2048 + 128 * a
                    return ten[rs, st0:st0 + 128]

                def bank_of(kind, a, b):
                    return b if kind == 1 else a // 4

                def accchunk(rs, kind, a, b=0):
                    t_ = accm[bank_of(kind, a, b) % 2]
                    if kind == 1:
                        return t_[rs, 128 * a:128 * a + 128]
                    q4 = a % 4
                    return t_[rs, :].rearrange("p (c q n) -> p c q n", c=4, q=4)[:, :, q4, :]

                blocks = []
                for c in range(16):
                    blocks.append((0, c, 0, c, (0, c, 0, 48 + c)))
                for bk in range(4):
                    for c4 in range(4):
                        bb = bk
                        prev = (0, c4, 3, 64 + c4) if bb == 0 else (1, c4, bb - 1, 16 + c4 * 4 + bb - 1)
                        blocks.append((1, c4, bb, 16 + c4 * 4 + bb, prev))
                    for b1 in range(4 * bk, 4 * bk + 4):
                        prev = (0, 15, 0, 68) if b1 == 0 else (1, b1 - 1, 0, 32 + b1 - 1)
                        blocks.append((2, b1, 0, 32 + b1, prev))
                vlist = []
                for c in range(16):
                    vlist.append((c, 1, 0, c, 0))
                    vlist.append((48 + c, 0, 0, c, 0))
                for c4 in range(4):
                    for bb in range(4):
                        vlist.append((16 + c4 * 4 + bb, 1, 1, c4, bb))
                    vlist.append((64 + c4, 0, 1, c4, 3))
                for b1 in range(16):
                    vlist.append((32 + b1, 1, 2, b1, 0))
                vlist.append((68, 0, 2, 15, 0))
                vlist.sort()

                it = 0
                vtpi = 0
                sti = 0
                pti = 0
                for Sb in (() if skipB else (1, 2)):
                    if Sb == 1:
                        S.op("pool", lambda e: e.tensor_scalar(Vc[:, 48:69, 64:128], Vc[:, 48:69, 64:128], flag[:, 0:1], None, ALU.mult),
                             reads=[flag, Vc], writes=[Vc])
                    else:
                        S.op("pool", lambda e: e.memset(Vc[:, 48:69, 64:128], 1.0), reads=[Vc], writes=[Vc])
                    for p in range(4):
                        bi = it % 2
                        q_, k_, v_, g_ = qs[bi], ksb[bi], vsb[bi], gsb[bi]
                        o0 = (Sb - 1) * 2048
                        S.group(bsem[bi], 4)
                        S.dma(bsem[bi], lambda e, q_=q_, p=p, o0=o0: e.dma_start(out=q_[:, :], in_=qT_d[p, :, o0:o0 + 2048]), writes=[q_])
                        S.dma(bsem[bi], lambda e, k_=k_, p=p, o0=o0: e.dma_start(out=k_[:, :], in_=kT_d[p, :, o0:o0 + 4096]), writes=[k_])
                        S.dma(bsem[bi], lambda e, v_=v_, p=p, o0=o0: e.dma_start(out=v_[:, :], in_=vT_d[p, :, o0:o0 + 4096]), writes=[v_])
                        S.dma(bsem[bi], lambda e, g_=g_, p=p, o0=o0: e.dma_start(out=g_[:, :], in_=gbT_d[p, :, o0:o0 + 2048]), writes=[g_])
                        for i0 in range(0, 69, 8):
                            n = min(8, 69 - i0)
                            tpv = vtp[vtpi % (LA + 1)]
                            vtpi += 1
                            for jx in range(n):
                                (idx, sbk, kind, a, b) = vlist[i0 + jx]
                                assert idx == i0 + jx
                                src = kchunk(v_, slice(0, 128), kind, sbk, a, b)
                                S.op("pe", lambda e, tpv=tpv, jx=jx, src=src: e.transpose(tpv[:, jx, :], src, ident[:, :]),
                                     reads=[v_, ident], writes=[tpv], sig=(jx == n - 1))
                            S.op("dve", lambda e, tpv=tpv, i0=i0, n=n: e.tensor_copy(Vc[:, i0:i0 + n, 0:64], tpv[:, 0:n, 0:64]),
                                 reads=[tpv], writes=[Vc])
                            S.op("act", lambda e, tpv=tpv, i0=i0, n=n: e.activation(out=Vc[:, i0:i0 + n, 128:192], in_=tpv[:, 0:n, 64:128], func=AF.Copy),
                                 reads=[tpv, Vc], writes=[Vc])
                        for hp in range(2):
                            tail_weight_jobs(2)
                            rs = slice(hp * 64, hp * 64 + 64)
                            lo = 0 if hp == 0 else 64
                            first = True
                            pend = []
                            npair = len(blocks) // 2

                            def do_pv(pair, ptile, sig_last, extra_reads=()):
                                for j in range(2):
                                    (kind, a, b, cidx, prev) = blocks[2 * pair + j]
                                    (psb, pa, pb_, pidx) = prev
                                    for half, vidx in ((0, pidx), (1, cidx)):
                                        lw = Vc[:, vidx, lo:lo + 128]
                                        rhs_full = ptile[:, j * 256 + half * 128: j * 256 + half * 128 + 128]
                                        if kind == 0:
                                            o16 = acc16p[:, (a % 4) * 128:(a % 4) * 128 + 128]
                                            S.op("pe", lambda e, lw=lw, o16=o16, rhs_full=rhs_full, half=half: e.matmul(
                                                o16, lhsT=lw, rhs=rhs_full, start=(half == 0), stop=(half == 1), skip_group_check=True),
                                                reads=[Vc, ptile] + list(extra_reads), writes=[acc16p], sig=(sig_last and j == 1 and half == 1))
                                            if half == 1 and a % 4 == 3:
                                                S.op("act", lambda e, a=a: e.activation(out=a16[:, (a - 3) * 128:(a + 1) * 128], in_=acc16p[:, :],
                                                                                      func=AF.Copy), reads=[acc16p], writes=[a16])
                                        else:
                                            oa = accchunk(slice(0, 128), kind, a, b)
                                            bk_ = bank_of(kind, a, b)
                                            am = accm[bk_ % 2]
                                            rr = rhs_full if kind == 1 else rhs_full.rearrange("p (x y) -> p x y", x=4)
                                            st_ = (kind == 1 and a == 0 and half == 0)
                                            S.op("pe", lambda e, lw=lw, oa=oa, rr=rr, st_=st_: e.matmul(oa, lhsT=lw, rhs=rr, start=st_, stop=False,
                                                                                                      skip_group_check=True),
                                                 reads=[Vc, ptile] + list(extra_reads), writes=[am], sig=(sig_last and j == 1 and half == 1))
                                            if kind == 2 and a % 4 == 3 and half == 1:
                                                S.op("act", lambda e, am=am, bk_=bk_: e.activation(out=rec[:, 512 * bk_:512 * bk_ + 512], in_=am[:, :],
                                                                                                  func=AF.Copy), reads=[am], writes=[rec])

                            for ss in range(npair // 2):
                                stbs = []
                                for pi_ in range(2):
                                    pair = 2 * ss + pi_
                                    stb = stp[sti % NST]
                                    sti += 1
                                    stbs.append(stb)
                                    for j in range(2):
                                        (kind, a, b, cidx, prev) = blocks[2 * pair + j]
                                        (psb, pa, pb_, pidx) = prev
                                        qa = chunk(q_, rs, kind, 0, a, b)
                                        kprev = kchunk(k_, rs, kind, psb, pa, pb_)
                                        kcur = kchunk(k_, rs, kind, 1, a, b)
                                        S.op("pe", lambda e, stb=stb, j=j, kprev=kprev, qa=qa: e.matmul(
                                            stb[:, j * 256:j * 256 + 128], lhsT=kprev, rhs=qa, start=True, stop=True, skip_group_check=True),
                                            reads=[k_, q_], writes=[stb], sig=False)
                                        last = (pi_ == 1 and j == 1)
                                        S.op("pe", lambda e, stb=stb, j=j, kcur=kcur, qa=qa: e.matmul(
                                            stb[:, j * 256 + 128:j * 256 + 256], lhsT=kcur, rhs=qa, start=True, stop=True, skip_group_check=True),
                                            reads=[k_, q_], writes=(list(stbs) if last else [stb]), sig=last)
                                ptiles = []
                                for pi_ in range(2):
                                    pair = 2 * ss + pi_
                                    stb = stbs[pi_]
                                    ptile = pt[pti % NPT]
                                    pti += 1
                                    ptiles.append(ptile)
                                    S.op("act", lambda e, stb=stb, ptile=ptile: e.activation(out=ptile[:, :], in_=stb[:, :], func=AF.Exp, scale=0.125),
                                         reads=[stb], writes=[ptile])
                                    kind = blocks[2 * pair][0]
                                    mk = masks[:, 2 * kind:2 * kind + 2, :].unsqueeze(1).broadcast_to([128, 2, 2, 128])
                                    pv4 = ptile[:, :].rearrange("p (j h n) -> p j h n", j=2, h=2)
                                    meng = "dve" if pi_ == 0 else "pool"
                                    S.op(meng, lambda e, pv4=pv4, mk=mk: e.tensor_tensor(out=pv4, in0=pv4, in1=mk, op=ALU.mult),
                                         reads=[ptile, masks], writes=[ptile])
                                pend.append((ss, ptiles))
                                if len(pend) > LAS:
                                    ss0, pts = pend.pop(0)
                                    do_pv(2 * ss0, pts[0], False)
                                    do_pv(2 * ss0 + 1, pts[1], True, extra_reads=[pts[0]])
                            while pend:
                                ss0, pts = pend.pop(0)
                                do_pv(2 * ss0, pts[0], False)
                                do_pv(2 * ss0 + 1, pts[1], True, extra_reads=[pts[0]])
                            num = slice(0, 64) if hp == 0 else slice(64, 128)
                            den = slice(64, 128) if hp == 0 else slice(0, 64)
                            a16v = a16[:, :].rearrange("p (cc c k i) -> p k c i cc", cc=4, c=4, k=4)
                            recv = rec[:, :].rearrange("p (k c i cc) -> p k c i cc", k=4, c=4, cc=4)
                            for kk in range(4):
                                S.op("dve", lambda e, kk=kk: e.tensor_tensor(out=recv[:, kk], in0=recv[:, kk], in1=a16v[:, kk], op=ALU.add),
                                     reads=[rec, a16], writes=[rec])
                            S.op("act", lambda e, num=num, den=den: e.activation(out=a16[num, :], in_=rec[den, :], func=AF.Ln), reads=[rec], writes=[a16])
                            S.op("act", lambda e, num=num: e.activation(out=a16[num, :], in_=a16[num, :], func=AF.Exp, scale=-1.0), reads=[a16], writes=[a16])
                            S.op("dve", lambda e, num=num: e.tensor_tensor(out=rec[num, :], in0=rec[num, :], in1=a16[num, :], op=ALU.mult),
                                 reads=[rec, a16], writes=[rec])
                            S.op("dve", lambda e, num=num, g_=g_, p=p, o0=o0: e.tensor_tensor(out=atT[num, p, o0 + 1280:o0 + 2048], in0=rec[num, 1280:2048],
                                                                                         in1=g_[num, 1280:2048], op=ALU.mult),
                                 reads=[rec, g_], writes=[atT_hi])
                            S.op("pool", lambda e, num=num, g_=g_, p=p, o0=o0: e.tensor_tensor(out=atT[num, p, o0:o0 + 1280], in0=rec[num, 0:1280],
                                                                                          in1=g_[num, 0:1280], op=ALU.mult),
                                 reads=[rec, g_], writes=[atT])
                        it += 1
                if dbg:
                    at_dbg = nc.dram_tensor("at_dbg", [128, 4, NTOK], BF, kind="ExternalOutput").ap()
                    gm_dbg = nc.dram_tensor("gm_dbg", [128, 4, NTOK], BF, kind="ExternalOutput").ap()
                    S.dma(dsem_c, lambda e: e.dma_start(out=at_dbg[:, :, :], in_=atT[:, :, :]), reads=[atT])
                    S.dma(dsem_c, lambda e: e.dma_start(out=gm_dbg[:, :, :], in_=gmT[:, :, :]), reads=[gmT])
                S.barrier()
                with nc.Block() as block:
                    S.emit(block)

            if stop == "B":
                return nc
            with ExitStack() as Cs:
                NS = 2
                xt2 = [sb(Cs, "xt2_%d" % i, [128, 1024], F32) for i in range(NS)]
                pt2 = [sb(Cs, "pt2_%d" % i, [128, 256], F32) for i in range(NS)]
                pbf = [sb(Cs, "pbf%d" % i, [128, 256], BF) for i in range(NS)]
                pT = [sb(Cs, "pT%d" % i, [128, 2, 128], BF) for i in range(NS)]
                h_ = [sb(Cs, "h%d" % i, [128, 1024], F32) for i in range(NS)]
                hsq = sb(Cs, "hsq", [128, 1024], BF)
                hst = [sb(Cs, "hst%d" % i, [128, 4], F32) for i in range(NS)]
                hn = [sb(Cs, "hn%d" % i, [128, 1024], BF) for i in range(NS)]
                hnT = [sb(Cs, "hnT%d" % i, [128, 8, 128], BF) for i in range(NS)]
                gate = [sb(Cs, "gate%d" % i, [128, 1024], F32) for i in range(NS)]
                ot = [sb(Cs, "ot%d" % i, [128, 1024], F32) for i in range(NS)]
                hp_ = [[ps(Cs, "hp%d_%d" % (i, g), [128, 512], F32) for g in range(2)] for i in range(NS)]
                gp_ = [[ps(Cs, "gp%d_%d" % (i, g), [128, 512], F32) for g in range(2)] for i in range(NS)]
                tpv8 = [[TV(gp_[i][g], (lambda i=i, g=g: gp_[i][g][:, :].bitcast(BF).rearrange("p (a t) -> p a t", a=8)))
                         for g in range(2)] for i in range(NS)]
                csem = [S.new_dma_sem() for _ in range(NS)]
                osem = [S.new_dma_sem() for _ in range(NS)]

                tiles = [(Sb, k, w) for Sb in (0, 1) for k in range(4) for w in range(4)]

                def rows(Sb, k, w):
                    return Sb * 2048 + 512 * k + w

                def load_c(i, slot):
                    (Sb, k, w) = tiles[i]
                    x_ = xt2[slot]
                    p_ = pt2[slot]
                    S.group(csem[slot], 2)
                    r0 = rows(Sb, k, w)
                    S.dma(csem[slot], lambda e, x_=x_, r0=r0: e.dma_start(
                        out=x_[:, :], in_=xh[NHALO + r0:NHALO + r0 + 509:4, :]), writes=[x_])
                    S.dma(csem[slot], lambda e, p_=p_, r0=r0: e.dma_start(
                        out=p_[:, :], in_=pown[r0:r0 + 509:4, :]), writes=[p_])

                def tail_chain(i, slot):
                    (Sb, k, w) = tiles[i]
                    L0 = Sb * 2048 + 512 * k + 128 * w
                    x_, p_, hh, hs, h_n, h_nT = xt2[slot], pt2[slot], h_[slot], hst[slot], hn[slot], hnT[slot]
                    pb_, p_T, gt, oo = pbf[slot], pT[slot], gate[slot], ot[slot]
                    hp, gp, tp = hp_[slot], gp_[slot], tpv8[slot]

                    def t0():
                        for g in range(2):
                            for e8 in range(8):
                                srcT = atT if e8 < 4 else gmT
                                S.op("pe", lambda e, g=g, e8=e8, srcT=srcT: e.matmul(
                                    hp[g][:, :], lhsT=srcT[:, e8 % 4, L0:L0 + 128], rhs=Wo[:, e8, g * 512:(g + 1) * 512],
                                    start=(e8 == 0), stop=(e8 == 7), skip_group_check=True),
                                    reads=[atT, atT_hi, gmT, Wo], writes=([hp[0], hp[1]] if g == 1 else [hp[g]]), sig=(g == 1 and e8 == 7))

                    def t1():
                        for g in range(2):
                            S.op("dve", lambda e, g=g: e.tensor_tensor(out=hh[:, g * 512:(g + 1) * 512], in0=hp[g][:, :],
                                                                    in1=x_[:, g * 512:(g + 1) * 512], op=ALU.add),
                                 reads=[hp[g], x_, hh], writes=[hh])
                        S.op("pool", lambda e: e.memset(hs[:, :], 0.0), writes=[hs])
                        S.op("pool", lambda e: e.tensor_copy(pb_[:, :], p_[:, :]), reads=[p_], writes=[pb_])
                        if i + NS < len(tiles):
                            load_c(i + NS, slot)

                    def t2():
                        S.op("act", lambda e: e.activation(out=hsq[:, :], in_=hh[:, :], func=AF.Square, accum_out=hs[:, 0:1]),
                             reads=[hh, hs], writes=[hsq, hs])
                        for kc in range(2):
                            S.op("pe", lambda e, kc=kc: e.transpose(tp[1][:, kc, :], pb_[:, kc * 128:(kc + 1) * 128], ident[:, :]),
                                 reads=[pb_, ident], writes=[tp[1]], sig=(kc == 1))

                    def t3():
                        S.op("dve", lambda e: e.tensor_scalar(hs[:, 1:2], hs[:, 0:1], 1.0 / 1024, EPS, ALU.mult, ALU.add), reads=[hs], writes=[hs])
                        S.op("act", lambda e: e.activation(out=p_T[:, :, :], in_=tp[1][:, 0:2, :], func=AF.Copy), reads=[tp[1]], writes=[p_T])

                    def t4():
                        S.op("pool", lambda e: e.tensor_tensor(out=hs[:, 2:3], in0=hs[:, 1:2], in1=mhalf[:, 0:1], op=ALU.pow),
                             reads=[hs, mhalf], writes=[hs])

                    def t5():
                        S.op("act", lambda e: e.activation(out=h_n[:, :], in_=hh[:, :], func=AF.Copy, scale=hs[:, 2:3]), reads=[hh, hs], writes=[h_n])

                    def t6():
                        for kc in range(8):
                            S.op("pe", lambda e, kc=kc: e.transpose(tp[0][:, kc, :], h_n[:, kc * 128:(kc + 1) * 128], ident[:, :]),
                                 reads=[h_n, ident], writes=[tp[0]], sig=(kc == 7))

                    def t7():
                        S.op("dve", lambda e: e.tensor_copy(h_nT[:, :, :], tp[0][:, :, :]), reads=[tp[0]], writes=[h_nT])

                    def t8():
                        for g in range(2):
                            for kc in range(8):
                                S.op("pe", lambda e, g=g, kc=kc: e.matmul(gp[g][:, :], lhsT=h_nT[:, kc, :], rhs=Wg[:, kc, g * 512:(g + 1) * 512],
                                                                           start=(kc == 0), stop=False, skip_group_check=True),
                                     reads=[h_nT, Wg], writes=[gp[g]], sig=False)
                            S.op("pe", lambda e, g=g: e.matmul(gp[g][:, :], lhsT=ones1[0:1, :], rhs=bg_b[0:1, g * 512:(g + 1) * 512],
                                                               start=False, stop=True, skip_group_check=True),
                                 reads=[h_nT, Wg, ones1, bg_b], writes=([gp[0], gp[1]] if g == 1 else [gp[g]]), sig=(g == 1))
                        for g in range(2):
                            for kc in range(2):
                                S.op("pe", lambda e, g=g, kc=kc: e.matmul(hp[g][:, :], lhsT=p_T[:, kc, :], rhs=Wp[:, kc, g * 512:(g + 1) * 512],
                                                                           start=(kc == 0), stop=(kc == 1), skip_group_check=True),
                                     reads=[p_T, Wp], writes=([hp[0], hp[1]] if g == 1 else [hp[g]]), sig=(g == 1 and kc == 1))

                    def t9():
                        for g in range(2):
                            S.op("act", lambda e, g=g: e.activation(out=gt[:, g * 512:(g + 1) * 512], in_=gp[g][:, :], func=AF.Sigmoid),
                                 reads=[gp[g], gt], writes=[gt])

                    def t10():
                        for g in range(2):
                            S.op("dve", lambda e, g=g: e.tensor_tensor(out=oo[:, g * 512:(g + 1) * 512], in0=hp[g][:, :],
                                                                      in1=gt[:, g * 512:(g + 1) * 512], op=ALU.mult),
                                 reads=[hp[g], gt, oo], writes=[oo])

                    def t11():
                        S.op("pool", lambda e: e.tensor_tensor(out=oo[:, :], in0=oo[:, :], in1=hh[:, :], op=ALU.add), reads=[oo, hh], writes=[oo])
                        S.group(osem[slot], 1)
                        r0 = rows(Sb, k, w)
                        S.dma(osem[slot], lambda e, r0=r0: e.dma_start(out=out[r0:r0 + 509:4, :], in_=oo[:, :]), reads=[oo])

                    return [t0, t1, t2, t3, t4, t5, t6, t7, t8, t9, t10, t11]

                for i0 in range(NS):
                    load_c(i0, i0)
                run_chains_g([(lambda slot, i=i: tail_chain(i, slot)) for i in range(len(tiles))], NS)
                S.barrier()
                with nc.Block() as block:
                    S.emit(block)
    return nc


_CACHE = {}


def kernel(x, p, positions, g_in, w_in, q_norm, k_norm, w_spatial, b_spatial,
           ln_v_g, ln_v_b, w_out, g_ple, w_ple_gate, b_ple_gate, w_ple_proj):
    f32 = np.float32
    x = np.asarray(x, f32)
    p = np.asarray(p, f32)[0]
    positions = np.asarray(positions, np.int32)
    ident, masks, invf, tril = _const_tables()
    if "nc" not in _CACHE:
        _CACHE["nc"] = build_nc()
    nc = _CACHE["nc"]

    def col8(v):
        return np.ascontiguousarray(np.asarray(v, f32).reshape(8, 128).T)

    shared = {
        "w_in": np.ascontiguousarray(np.asarray(w_in, f32)[0]),
        "w_out": np.ascontiguousarray(np.asarray(w_out, f32)[0]),
        "w_gate": np.ascontiguousarray(np.asarray(w_ple_gate, f32)[0]),
        "w_pp": np.ascontiguousarray(np.asarray(w_ple_proj, f32)[0]),
        "gin": col8(np.asarray(g_in)[0]),
        "gple": col8(np.asarray(g_ple)[0]),
        "qn_b": np.ascontiguousarray(np.broadcast_to(np.asarray(q_norm, f32)[0][None, :], (128, 64))),
        "kn_b": np.ascontiguousarray(np.broadcast_to(np.asarray(k_norm, f32)[0][None, :], (128, 64))),
        "lng": np.ascontiguousarray(np.asarray(ln_v_g, f32)[0].reshape(4, 128).T),
        "lnb": np.ascontiguousarray(np.asarray(ln_v_b, f32)[0].reshape(4, 128).T),
        "bsb": np.ascontiguousarray(np.broadcast_to(np.asarray(b_spatial, f32)[0].reshape(1, 512), (128, 512))),
        "wsT": np.ascontiguousarray(np.asarray(w_spatial, f32)[0].transpose(2, 0, 1).reshape(128, 512)),
        "bgate": np.ascontiguousarray(np.asarray(b_ple_gate, f32)[0].reshape(1, 1024)),
        "ident": ident, "masks": np.ascontiguousarray(masks.reshape(128, 768)), "invf": invf, "tril": tril,
    }
    in_maps = []
    for core in range(8):
        b, half = core // 2, core % 2
        s0 = half * NTOK
        if half == 0:
            xhalo = np.zeros((NHALO, 1024), f32)
            phalo = np.zeros((NHALO,), np.int32)
        else:
            xhalo = x[b, s0 - NHALO:s0]
            phalo = positions[b, s0 - NHALO:s0]
        xh = np.concatenate([xhalo, x[b, s0:s0 + NTOK]], axis=0)
        pos = np.concatenate([phalo, positions[b, s0:s0 + NTOK]], axis=0)
        m = dict(shared)
        m["xh"] = np.ascontiguousarray(xh)
        m["pown"] = np.ascontiguousarray(p[b, s0:s0 + NTOK])
        m["pos_t"] = np.ascontiguousarray(pos.reshape(48, 128).T)
        m["flag"] = np.full((128, 1), float(half), f32)
        in_maps.append(m)
    res = run_bass_kernel_spmd(nc, in_maps, core_ids=list(range(8)))
    outp = np.empty((4, 8192, 1024), f32)
    for core in range(8):
        b, half = core // 2, core % 2
        outp[b, half * NTOK:(half + 1) * NTOK] = res.results[core]["out"]
    return outp
```

```python
import math
from contextlib import ExitStack

import numpy as np
import ml_dtypes

import concourse.bass as bass
import concourse.mybir as mybir
from concourse.bass_utils import run_bass_kernel_spmd

F32 = mybir.dt.float32
BF = mybir.dt.bfloat16
I32 = mybir.dt.int32
AF = mybir.ActivationFunctionType
ALU = mybir.AluOpType
AX = mybir.AxisListType

ENGS = ("pe", "act", "dve", "pool", "sp")
EPS = 1e-6
NTOK = 4096
NHALO = 2048
NALL = NTOK + NHALO
PI = math.pi


class Buf:
    __slots__ = ("w", "r")

    def __init__(self):
        self.w = None
        self.r = []


class T:
    def __init__(self, t):
        self.t = t
        self.b = Buf()

    def __getitem__(self, k):
        return self.t[k]


class TV(T):
    def __init__(self, base, fn):
        self.b = base.b
        self.fn = fn

    def __getitem__(self, k):
        return self.fn()[k]


class Sched:
    def __init__(self, nc):
        self.nc = nc
        self.ops = {e: [] for e in ENGS}
        self.cnt = {}
        self.sems = {}
        self.seen = {e: {} for e in ENGS}
        self.dma_free = []
        self.gtarget = {}

    def open(self, stack, n_dma_sems):
        for e in ENGS:
            self.sems[e] = stack.enter_context(self.nc.semaphore("s_" + e))
            self.cnt[e] = 0
        for i in range(n_dma_sems):
            nm = "d%d" % i
            self.sems[nm] = stack.enter_context(self.nc.semaphore("s_" + nm))
            self.cnt[nm] = 0
            self.dma_free.append(nm)

    def new_dma_sem(self):
        return self.dma_free.pop(0)

    def _deps(self, eng, reads, writes):
        deps = {}
        for b in reads:
            if b.w is not None:
                s, v = b.w
                deps[s] = max(deps.get(s, 0), v)
        for b in writes:
            if b.w is not None:
                s, v = b.w
                deps[s] = max(deps.get(s, 0), v)
            for (s, v) in b.r:
                deps[s] = max(deps.get(s, 0), v)
        waits = []
        for s, v in deps.items():
            if self.seen[eng].get(s, 0) >= v:
                continue
            self.seen[eng][s] = v
            waits.append((s, v))
        return waits

    def _reg(self, tok, reads, writes):
        for b in reads:
            b.r.append(tok)
            if len(b.r) > 64:
                best = {}
                for s, v in b.r:
                    best[s] = max(best.get(s, 0), v)
                b.r = list(best.items())
        for b in writes:
            b.w = tok
            b.r = []

    def op(self, eng, fn, reads=(), writes=(), sig=True):
        reads = [x.b if isinstance(x, T) else x for x in reads]
        writes = [x.b if isinstance(x, T) else x for x in writes]
        waits = self._deps(eng, reads, writes)
        if sig:
            self.cnt[eng] += 1
            self._reg((eng, self.cnt[eng]), reads, writes)
        self.ops[eng].append((waits, fn, (eng, 1) if sig else None))

    def group(self, sem, n):
        self.gtarget[sem] = self.cnt[sem] + 16 * n

    def dma(self, sem, fn, reads=(), writes=(), eng="sp"):
        reads = [x.b if isinstance(x, T) else x for x in reads]
        writes = [x.b if isinstance(x, T) else x for x in writes]
        tgt = self.gtarget.get(sem, 0)
        saved = self.seen[eng].get(sem, 0)
        if tgt > self.cnt[sem]:
            self.seen[eng][sem] = max(saved, tgt)
        waits = self._deps(eng, reads, writes)
        self.seen[eng][sem] = max(saved, max([v for (s_, v) in waits if s_ == sem] + [saved]))
        self.cnt[sem] += 16
        tokv = tgt if tgt >= self.cnt[sem] else self.cnt[sem]
        self._reg((sem, tokv), reads, writes)
        self.ops[eng].append((waits, fn, (sem, 16)))

    def wait_all(self, eng):
        waits = []
        for s, v in self.cnt.items():
            if s == eng:
                continue
            if v > 0 and self.seen[eng].get(s, 0) < v:
                self.seen[eng][s] = v
                waits.append((s, v))
        self.ops[eng].append((waits, None, None))

    def barrier(self):
        for e in ENGS:
            self.wait_all(e)

    def emit(self, block):
        def run(ename):
            ops = self.ops[ename]

            def body(h):
                for waits, fn, inc in ops:
                    for s, v in waits:
                        h.wait_ge(self.sems[s], v)
                    if fn is None:
                        continue
                    ins = fn(h)
                    if inc is not None:
                        ins.then_inc(self.sems[inc[0]], inc[1])
            return body

        block.tensor(run("pe"))
        block.scalar(run("act"))
        block.vector(run("dve"))
        block.gpsimd(run("pool"))
        block.sync(run("sp"))
        self.ops = {e: [] for e in ENGS}


def run_chains_g(makers, W):
    pending = list(makers)
    active = []
    free = list(range(W))
    while pending or active:
        while pending and free:
            slot = free.pop(0)
            active.append([slot, pending.pop(0)(slot), 0])
        for a in list(active):
            a[1][a[2]]()
            a[2] += 1
            if a[2] == len(a[1]):
                active.remove(a)
                free.append(a[0])


def _const_tables():
    ident = np.eye(128, dtype=np.float32).astype(ml_dtypes.bfloat16)
    n = np.arange(128)
    pos16 = n
    pos4 = n
    pos1 = 4 * (n % 32) + n // 32
    masks = np.zeros((128, 6, 128), np.float32)
    for br, pos in enumerate((pos16, pos4, pos1)):
        pk = n[:, None]
        pq = pos[None, :]
        masks[:, 2 * br + 0, :] = (pk >= pq)
        masks[:, 2 * br + 1, :] = (pk <= pq)
    masks = masks.astype(ml_dtypes.bfloat16)
    half = 32
    invf = np.exp(-math.log(10000.0) * np.arange(half, dtype=np.float32) / half).astype(np.float32)
    invf = np.broadcast_to(invf[None, :], (128, 32)).copy()
    tril = (n[:, None] <= n[None, :]).astype(np.float32)
    return ident, masks, invf, tril


def build_nc(stop=None, dbg=False, skipB=False):
    nc = bass.Bass("TRN2", target_bir_lowering=False)
    sk = "ExternalOutput"

    def din(name, shape, dt=F32):
        return nc.dram_tensor(name, shape, dt, kind="ExternalInput").ap()

    xh = din("xh", [NALL, 1024])
    pown = din("pown", [NTOK, 256])
    pos_t = din("pos_t", [128, 48], I32)
    flag_d = din("flag", [128, 1])
    w_in = din("w_in", [1024, 3584])
    w_out = din("w_out", [1024, 1024])
    w_gate = din("w_gate", [1024, 1024])
    w_pp = din("w_pp", [256, 1024])
    gin_d = din("gin", [128, 8])
    gple_d = din("gple", [128, 8])
    qn_d = din("qn_b", [128, 64])
    kn_d = din("kn_b", [128, 64])
    lng_d = din("lng", [128, 4])
    lnb_d = din("lnb", [128, 4])
    bsb_d = din("bsb", [128, 512])
    wsT_d = din("wsT", [128, 512])
    bgate_d = din("bgate", [1, 1024])
    ident_d = din("ident", [128, 128], BF)
    masks_d = din("masks", [128, 768], BF)
    invf_d = din("invf", [128, 32])
    tril_d = din("tril", [128, 128])
    out = nc.dram_tensor("out", [NTOK, 1024], F32, kind="ExternalOutput").ap()

    kT_d = nc.dram_tensor("kT_d", [4, 128, NALL], BF, kind=sk).ap()
    vT_d = nc.dram_tensor("vT_d", [4, 128, NALL], BF, kind=sk).ap()
    qT_d = nc.dram_tensor("qT_d", [4, 128, NTOK], BF, kind=sk).ap()
    gbT_d = nc.dram_tensor("gbT_d", [4, 128, NTOK], BF, kind=sk).ap()

    with ExitStack() as G:
        S = Sched(nc)
        S.open(G, 64)

        def sb(stack, name, shape, dt):
            return T(stack.enter_context(nc.sbuf_tensor("sb_" + name, shape, dt)))

        def ps(stack, name, shape, dt):
            return T(stack.enter_context(nc.psum_tensor("ps_" + name, shape, dt)))

        ident = sb(G, "ident", [128, 128], BF)
        masks = sb(G, "masks", [128, 6, 128], BF)
        flag = sb(G, "flag", [128, 1], F32)
        gmT = sb(G, "gmT", [128, 4, NTOK], BF)
        ones_bf = sb(G, "ones_bf", [128, 128], BF)
        dsem_c = S.new_dma_sem()

        def ld(dst, dst_ap, src_ap, sem=None):
            S.dma(sem or S.new_dma_sem(), lambda e: e.dma_start(out=dst_ap, in_=src_ap), writes=[dst])

        ld(ident, ident[:, :], ident_d[:, :])
        ld(masks, masks[:, :, :], masks_d[:, :].rearrange("p (a b) -> p a b", a=6))
        ld(flag, flag[:, :], flag_d[:, :])
        S.op("pool", lambda e: e.memset(ones_bf[:, :], 1.0), writes=[ones_bf])
        mhalf = sb(G, "mhalf", [128, 8], F32)
        S.op("pool", lambda e: e.memset(mhalf[:, :], -0.5), writes=[mhalf])

        with ExitStack() as A:
            Wb = sb(A, "Wb", [128, 8, 3584], BF)
            gin = sb(A, "gin", [128, 8], F32)
            qnb = sb(A, "qnb", [128, 64], F32)
            knb = sb(A, "knb", [128, 64], F32)
            gq1 = sb(A, "gq1", [128, 64], F32)
            gq2 = sb(A, "gq2", [128, 64], F32)
            gk1 = sb(A, "gk1", [128, 64], F32)
            gk2 = sb(A, "gk2", [128, 64], F32)
            lng = sb(A, "lng", [128, 4], F32)
            lnb = sb(A, "lnb", [128, 4], F32)
            wsb = sb(A, "wsb", [128, 4, 128], BF)
            tril = sb(A, "tril", [128, 128], F32)
            biasT = sb(A, "biasT", [128, 4, 128], F32)
            invf = sb(A, "invf", [128, 32], F32)
            posi = sb(A, "posi", [128, 48], I32)
            posf = sb(A, "posf", [128, 48], F32)

            xt = [sb(A, "xt%d" % i, [128, 1024], F32) for i in range(3)]
            xsq = sb(A, "xsq", [128, 1024], BF)
            xn = [sb(A, "xn%d" % i, [128, 1024], BF) for i in range(2)]
            xnT = [sb(A, "xnT%d" % i, [128, 8, 512], BF) for i in range(2)]
            st1 = [sb(A, "st1_%d" % i, [128, 4], F32) for i in range(3)]
            ang = sb(A, "ang", [128, 4, 32], F32)
            ni = sb(A, "ni", [128, 4, 32], I32)
            nf = sb(A, "nf", [128, 4, 32], F32)
            rr_ = sb(A, "rr_", [128, 4, 32], F32)
            arg = sb(A, "arg", [128, 2, 4, 32], F32)
            cs = sb(A, "cs", [128, 2, 4, 32], F32)
            Gq = sb(A, "Gq", [128, 2, 4, 64], F32)
            Gk = sb(A, "Gk", [128, 2, 4, 64], F32)
            NSET = 6
            sets = []
            for i in range(NSET):
                sets.append(dict(f0=sb(A, "f0_%d" % i, [128, 512], F32), f1=sb(A, "f1_%d" % i, [128, 512], F32),
                                 f2=sb(A, "f2_%d" % i, [128, 512], F32), b0=sb(A, "b0_%d" % i, [128, 512], BF),
                                 s8a=sb(A, "s8a_%d" % i, [128, 8], F32), s8b=sb(A, "s8b_%d" % i, [128, 8], F32),
                                 yst=sb(A, "yst_%d" % i, [128, 4], F32)))
            bsb = TV(sets[0]["f0"], lambda: sets[0]["f0"][:, :].rearrange("p (g t) -> p g t", g=4))
            wsT = TV(sets[0]["f1"], lambda: sets[0]["f1"][:, :].rearrange("p (g t) -> p g t", g=4))
            uT = sb(A, "uT", [128, 4, 512], BF)
            gaT = sb(A, "gaT", [128, 4, 512], BF)
            w1 = uT
            wst = [TV(uT, lambda: uT[:, :, :].rearrange("p a l -> p (a l)").bitcast(F32)[:, 0:896]),
                   TV(gaT, lambda: gaT[:, :, :].rearrange("p a l -> p (a l)").bitcast(F32)[:, 0:896])]
            kst = [sb(A, "kst0", [128, 4, 512], BF)] * 2
            qst = [sb(A, "qst0", [128, 4, 512], BF)] * 2
            vst = [sb(A, "vst0", [128, 4, 512], BF)] * 2
            gst = [sb(A, "gst0", [128, 4, 512], BF)] * 2

            tpx = [ps(A, "tpx%d" % i, [128, 8, 128], BF) for i in range(2)]
            pm = [ps(A, "pm%d" % i, [128, 512], F32) for i in range(6)]
            pmain = pm
            paux = pm
            tpq = [TV(pm[i], (lambda i=i: pm[i][:, :].bitcast(BF).rearrange("p (a t) -> p a t", a=8))) for i in range(6)]

            for (dst, src) in ((gin, gin_d), (qnb, qn_d), (knb, kn_d), (lng, lng_d), (lnb, lnb_d),
                               (tril, tril_d), (invf, invf_d), (posi, pos_t)):
                ld(dst, dst[:, :], src[:, :])
            ld(bsb, bsb[:, :, :], bsb_d[:, :].rearrange("p (g t) -> p g t", g=4))
            ld(wsT, wsT[:, :, :], wsT_d[:, :].rearrange("p (g t) -> p g t", g=4))
            S.op("dve", lambda e: e.tensor_copy(posf[:, :], posi[:, :]), reads=[posi], writes=[posf])
            for (src, g1, g2) in ((qnb, gq1, gq2), (knb, gk1, gk2)):
                S.op("dve", lambda e, src=src, g1=g1: e.tensor_scalar(g1[:, :], src[:, :], 1.0, None, ALU.mult),
                     reads=[src], writes=[g1])
                S.op("dve", lambda e, src=src, g2=g2: e.tensor_scalar(g2[:, 0:32], src[:, 32:64], -1.0, None, ALU.mult),
                     reads=[src], writes=[g2])
                S.op("dve", lambda e, src=src, g2=g2: e.tensor_scalar(g2[:, 32:64], src[:, 0:32], 1.0, None, ALU.mult),
                     reads=[src, g2], writes=[g2])
            S.op("dve", lambda e: e.tensor_tensor(out=wsb[:, :, :], in0=wsT[:, :, :],
                                                  in1=tril[:, :].unsqueeze(1).broadcast_to([128, 4, 128]), op=ALU.mult),
                 reads=[wsT, tril], writes=[wsb])
            for g in range(4):
                S.op("pe", lambda e, g=g: e.matmul(pm[0][:, g * 128:(g + 1) * 128], lhsT=ones_bf[:, :], rhs=wsb[:, g, :],
                                                   start=True, stop=True, skip_group_check=True),
                     reads=[ones_bf, wsb], writes=[pm[0]])
            for g in range(4):
                S.op("dve", lambda e, g=g: e.scalar_tensor_tensor(out=biasT[:, g, :], in0=pm[0][:, g * 128:(g + 1) * 128],
                                                                  scalar=lnb[:, g:g + 1], in1=bsb[:, g, :],
                                                                  op0=ALU.mult, op1=ALU.add),
                     reads=[pm[0], lnb, bsb, biasT], writes=[biasT])

            wsem = [S.new_dma_sem() for _ in range(2)]
            cnt = 0
            for half in range(4):
                for kc in range(8):
                    stg = wst[cnt % 2]
                    S.dma(wsem[cnt % 2], lambda e, stg=stg, kc=kc, half=half: e.dma_start(
                        out=stg[:, :], in_=w_in[kc * 128:(kc + 1) * 128, half * 896:(half + 1) * 896]), writes=[stg])
                    if cnt % 2 == 0:
                        S.op("dve", lambda e, stg=stg, kc=kc, half=half: e.tensor_scalar(
                            Wb[:, kc, half * 896:(half + 1) * 896], stg[:, :], gin[:, kc:kc + 1], None, ALU.mult),
                            reads=[stg, gin], writes=[Wb])
                    else:
                        S.op("act", lambda e, stg=stg, kc=kc, half=half: e.activation(
                            out=Wb[:, kc, half * 896:(half + 1) * 896], in_=stg[:, :], func=AF.Copy,
                            scale=gin[:, kc:kc + 1]), reads=[stg, gin], writes=[Wb])
                    cnt += 1

            xsem = [S.new_dma_sem() for _ in range(3)]
            stsem = [S.new_dma_sem() for _ in range(2)]
            pmi = [0]

            def next_pm():
                b = pm[pmi[0] % 6]
                pmi[0] += 1
                return b

            def load_x(ti):
                d = xt[ti % 3]
                S.dma(xsem[ti % 3], lambda e, d=d, ti=ti: e.dma_start(out=d[:, :], in_=xh[ti * 128:(ti + 1) * 128, :]),
                      writes=[d])

            def rr(x):
                return x[:, :].rearrange("p (h d) -> p h d", h=8)

            def qk_chain(kind, B, tl, slot):
                sc = sets[slot]
                f0, f1, f2, b0, s8a, s8b = sc["f0"], sc["f1"], sc["f2"], sc["b0"], sc["s8a"], sc["s8b"]
                xb = xnT[B % 2]
                col0 = 512 if kind == "k" else 0
                G_ = Gk if kind == "k" else Gq
                dst = (kst if kind == "k" else qst)[B % 2]
                tpb = tpq[slot]
                permute = kind == "q"
                st = {}

                def t0():
                    pb = st["pb"] = pmain[slot]
                    for kc in range(8):
                        S.op("pe", lambda e, kc=kc: e.matmul(pb[:, :], lhsT=xb[:, kc, tl * 128:(tl + 1) * 128], rhs=Wb[:, kc, col0:col0 + 512],
                                                             start=(kc == 0), stop=(kc == 7), skip_group_check=True),
                             reads=[Wb, xb], writes=[pb], sig=(kc == 7))

                def t1_():
                    pb = st["pb"]
                    S.op("act", lambda e: e.activation(out=f0[:, :], in_=pb[:, :], func=AF.Square), reads=[pb], writes=[f0])

                def t2_():
                    S.op("dve", lambda e: e.reduce_sum(out=s8a[:, :], in_=rr(f0), axis=AX.X), reads=[f0], writes=[s8a])
                    S.op("dve", lambda e: e.tensor_scalar(s8b[:, :], s8a[:, :], 1.0 / 64, EPS, ALU.mult, ALU.add), reads=[s8a], writes=[s8b])

                def t3_():
                    S.op("pool", lambda e: e.tensor_tensor(out=s8b[:, :], in0=s8b[:, :], in1=mhalf[:, :], op=ALU.pow),
                         reads=[s8b, mhalf], writes=[s8b])

                def t4_():
                    pb = st["pb"]
                    S.op("dve", lambda e: e.tensor_tensor(out=rr(f1), in0=rr(pb), in1=s8b[:, :].unsqueeze(2).broadcast_to([128, 8, 64]), op=ALU.mult),
                         reads=[pb, s8b], writes=[f1])

                def t5_():
                    ga = G_[:, 0, tl, :].unsqueeze(1).broadcast_to([128, 8, 64])
                    gb0 = G_[:, 1, tl, 0:32].unsqueeze(1).broadcast_to([128, 8, 32])
                    gb1 = G_[:, 1, tl, 32:64].unsqueeze(1).broadcast_to([128, 8, 32])
                    S.op("dve", lambda e: e.tensor_tensor(out=rr(f0), in0=rr(f1), in1=ga, op=ALU.mult), reads=[f1, G_], writes=[f0])
                    S.op("pool", lambda e: e.tensor_tensor(out=rr(f2)[:, :, 0:32], in0=rr(f1)[:, :, 32:64], in1=gb0, op=ALU.mult),
                         reads=[f1, G_], writes=[f2])
                    S.op("pool", lambda e: e.tensor_tensor(out=rr(f2)[:, :, 32:64], in0=rr(f1)[:, :, 0:32], in1=gb1, op=ALU.mult),
                         reads=[f1, G_, f2], writes=[f2])

                def t6_():
                    S.op("dve", lambda e: e.tensor_tensor(out=b0[:, :], in0=f0[:, :], in1=f2[:, :], op=ALU.add), reads=[f0, f2], writes=[b0])

                def t7_():
                    for pr in range(4):
                        S.op("pe", lambda e, pr=pr: e.transpose(tpb[:, pr, :], b0[:, pr * 128:(pr + 1) * 128], ident[:, :]),
                             reads=[b0, ident], writes=[tpb], sig=(pr == 3))

                def t8_():
                    if permute:
                        src = tpb[:, 0:4, :].rearrange("p a (n c) -> p a c n", c=4)
                        dsta = dst[:, :, :].rearrange("p a (c t n) -> p a c t n", c=4, t=4)[:, :, :, tl, :]
                    else:
                        src = tpb[:, 0:4, :]
                        dsta = dst[:, :, tl * 128:(tl + 1) * 128]
                    S.op("act", lambda e: e.activation(out=dsta, in_=src, func=AF.Copy), reads=[tpb], writes=[dst])

                return [t0, t1_, t2_, t3_, t4_, t5_, t6_, t7_, t8_]

            def va_chain(B, tl, slot):
                sc = sets[slot]
                yv, svv, y_n, yst = sc["f1"], sc["f2"], sc["b0"], sc["yst"]
                xb = xnT[B % 2]
                st = {}

                def t0():
                    pb = st["pb"] = pmain[slot]
                    for kc in range(8):
                        S.op("pe", lambda e, kc=kc: e.matmul(pb[:, :], lhsT=xb[:, kc, tl * 128:(tl + 1) * 128], rhs=Wb[:, kc, 2560:3072],
                                                             start=(kc == 0), stop=(kc == 7), skip_group_check=True),
                             reads=[Wb, xb], writes=[pb], sig=(kc == 7))
                    S.op("pool", lambda e: e.memset(yst[:, :], 0.0), writes=[yst])

                def t1_():
                    pb = st["pb"]
                    S.op("act", lambda e: e.activation(out=yv[:, :], in_=pb[:, :], func=AF.Gelu_apprx_tanh, accum_out=yst[:, 0:1]),
                         reads=[pb, yst], writes=[yv, yst])

                def t2_():
                    S.op("dve", lambda e: e.tensor_scalar(yst[:, 1:2], yst[:, 0:1], -1.0 / 512, None, ALU.mult), reads=[yst], writes=[yst])

                def t3_():
                    S.op("act", lambda e: e.activation(out=xsq[:, 0:512], in_=yv[:, :], func=AF.Square, bias=yst[:, 1:2], accum_out=yst[:, 2:3]),
                         reads=[yv, yst], writes=[xsq, yst])

                def t4_():
                    S.op("dve", lambda e: e.tensor_scalar(yst[:, 3:4], yst[:, 2:3], 1.0 / 512, EPS, ALU.mult, ALU.add), reads=[yst], writes=[yst])

                def t5_():
                    S.op("pool", lambda e: e.tensor_tensor(out=yst[:, 3:4], in0=yst[:, 3:4], in1=mhalf[:, 0:1], op=ALU.pow),
                         reads=[yst, mhalf], writes=[yst])

                def t6_():
                    S.op("dve", lambda e: e.tensor_scalar(y_n[:, :], yv[:, :], yst[:, 1:2], yst[:, 3:4], ALU.add, ALU.mult),
                         reads=[yv, yst], writes=[y_n])

                def t7_():
                    pb2 = st["pb2"] = paux[slot]
                    for g in range(4):
                        S.op("pe", lambda e, g=g: e.matmul(pb2[:, g * 128:(g + 1) * 128], lhsT=y_n[:, g * 128:(g + 1) * 128], rhs=wsb[:, g, :],
                                                           start=True, stop=True, skip_group_check=True),
                             reads=[y_n, wsb], writes=[pb2], sig=(g == 3))

                def t8_():
                    pb2 = st["pb2"]
                    sv3 = svv[:, :].rearrange("p (g t) -> p g t", g=4)
                    for g in range(4):
                        S.op("dve", lambda e, g=g: e.scalar_tensor_tensor(out=sv3[:, g, :], in0=pb2[:, g * 128:(g + 1) * 128],
                                                                          scalar=lng[:, g:g + 1], in1=biasT[:, g, :], op0=ALU.mult, op1=ALU.add),
                             reads=[pb2, lng, biasT, svv], writes=[svv])

                def t9_():
                    own0 = (B - 4) * 512
                    dsta = gmT[:, :, own0:own0 + 512].rearrange("p g (c t n) -> p g c t n", c=4, t=4)[:, :, :, tl, :]
                    in0 = svv[:, :].rearrange("p (g n c) -> p g c n", g=4, c=4)
                    in1 = w1[:, :, tl * 128:(tl + 1) * 128].rearrange("p g (n c) -> p g c n", c=4)
                    S.op("dve", lambda e: e.tensor_tensor(out=dsta, in0=in0, in1=in1, op=ALU.mult), reads=[svv, w1], writes=[gmT])

                return [t0, t1_, t2_, t3_, t4_, t5_, t6_, t7_, t8_, t9_]

            def stage1_chain(ti, slot):
                B, tl = ti // 4, ti % 4
                x_, s_, x_n, tp, xb = xt[ti % 3], st1[ti % 3], xn[ti % 2], tpx[ti % 2], xnT[B % 2]

                def t0():
                    S.op("pool", lambda e: e.memset(s_[:, :], 0.0), writes=[s_])
                    S.op("act", lambda e: e.activation(out=xsq[:, :], in_=x_[:, :], func=AF.Square, accum_out=s_[:, 0:1]),
                         reads=[x_, s_], writes=[xsq, s_])

                def t1_():
                    S.op("dve", lambda e: e.tensor_scalar(s_[:, 1:2], s_[:, 0:1], 1.0 / 1024, EPS, ALU.mult, ALU.add), reads=[s_], writes=[s_])

                def t2_():
                    S.op("pool", lambda e: e.tensor_tensor(out=s_[:, 2:3], in0=s_[:, 1:2], in1=mhalf[:, 0:1], op=ALU.pow),
                         reads=[s_, mhalf], writes=[s_])

                def t3_():
                    S.op("act", lambda e: e.activation(out=x_n[:, :], in_=x_[:, :], func=AF.Copy, scale=s_[:, 2:3]), reads=[x_, s_], writes=[x_n])
                    if ti + 3 < 48:
                        load_x(ti + 3)

                def t4_():
                    for kc in range(8):
                        S.op("pe", lambda e, kc=kc: e.transpose(tp[:, kc, :], x_n[:, kc * 128:(kc + 1) * 128], ident[:, :]),
                             reads=[x_n, ident], writes=[tp], sig=(kc == 7))

                def t5_():
                    S.op("act", lambda e: e.activation(out=xb[:, :, tl * 128:(tl + 1) * 128], in_=tp[:, :, :], func=AF.Copy), reads=[tp], writes=[xb])

                return [t0, t1_, t2_, t3_, t4_, t5_]

            def run_chains(makers, W):
                pending = list(makers)
                active = []
                free = list(range(W))
                while pending or active:
                    while pending and free:
                        slot = free.pop(0)
                        active.append([slot, pending.pop(0)(slot), 0])
                    for a in list(active):
                        a[1][a[2]]()
                        a[2] += 1
                        if a[2] == len(a[1]):
                            active.remove(a)
                            free.append(a[0])

            def rope_tables(B):
                halo = B < 4
                C1 = 6.28125
                C2 = 2 * PI - 6.28125
                S.op("dve", lambda e: e.tensor_tensor(out=ang[:, :, :], in0=invf[:, :].unsqueeze(1).broadcast_to([128, 4, 32]),
                                                      in1=posf[:, 4 * B:4 * B + 4].unsqueeze(2).broadcast_to([128, 4, 32]), op=ALU.mult),
                     reads=[invf, posf, ang], writes=[ang])
                S.op("dve", lambda e: e.tensor_scalar(ni[:, :, :], ang[:, :, :], 1.0 / (2 * PI), None, ALU.mult), reads=[ang, ni], writes=[ni])
                S.op("dve", lambda e: e.tensor_copy(nf[:, :, :], ni[:, :, :]), reads=[ni, nf], writes=[nf])
                S.op("dve", lambda e: e.scalar_tensor_tensor(out=rr_[:, :, :], in0=nf[:, :, :], scalar=-C1, in1=ang[:, :, :],
                                                             op0=ALU.mult, op1=ALU.add), reads=[nf, ang, rr_], writes=[rr_])
                S.op("dve", lambda e: e.scalar_tensor_tensor(out=rr_[:, :, :], in0=nf[:, :, :], scalar=-C2, in1=rr_[:, :, :],
                                                             op0=ALU.mult, op1=ALU.add), reads=[nf, rr_], writes=[rr_])
                S.op("dve", lambda e: e.tensor_scalar(nf[:, :, :], rr_[:, :, :], PI, 2 * PI, ALU.is_gt, ALU.mult), reads=[rr_, nf], writes=[nf])
                S.op("dve", lambda e: e.tensor_tensor(out=arg[:, 0, :, :], in0=rr_[:, :, :], in1=nf[:, :, :], op=ALU.subtract),
                     reads=[rr_, nf, arg], writes=[arg])
                S.op("dve", lambda e: e.tensor_scalar(rr_[:, :, :], arg[:, 0, :, :], 0.5 * PI, None, ALU.add), reads=[arg, rr_], writes=[rr_])
                S.op("dve", lambda e: e.tensor_scalar(nf[:, :, :], rr_[:, :, :], PI, 2 * PI, ALU.is_gt, ALU.mult), reads=[rr_, nf], writes=[nf])
                S.op("dve", lambda e: e.tensor_tensor(out=arg[:, 1, :, :], in0=rr_[:, :, :], in1=nf[:, :, :], op=ALU.subtract),
                     reads=[rr_, nf, arg], writes=[arg])
                S.op("act", lambda e: e.activation(out=cs[:, :, :, :], in_=arg[:, :, :, :], func=AF.Sin), reads=[arg, cs], writes=[cs])
                for (G_, g1, g2, need) in ((Gq, gq1, gq2, not halo), (Gk, gk1, gk2, True)):
                    if not need:
                        continue
                    cos2 = cs[:, 1, :, :].unsqueeze(2).broadcast_to([128, 4, 2, 32])
                    sin2 = cs[:, 0, :, :].unsqueeze(2).broadcast_to([128, 4, 2, 32])
                    S.op("dve", lambda e, G_=G_, g1=g1, cos2=cos2: e.tensor_tensor(
                        out=G_[:, 0, :, :].rearrange("p t (a d) -> p t a d", a=2), in0=cos2,
                        in1=g1[:, :].rearrange("p (a d) -> p a d", a=2).unsqueeze(1).broadcast_to([128, 4, 2, 32]),
                        op=ALU.mult), reads=[cs, g1, G_], writes=[G_])
                    S.op("dve", lambda e, G_=G_, g2=g2, sin2=sin2: e.tensor_tensor(
                        out=G_[:, 1, :, :].rearrange("p t (a d) -> p t a d", a=2), in0=sin2,
                        in1=g2[:, :].rearrange("p (a d) -> p a d", a=2).unsqueeze(1).broadcast_to([128, 4, 2, 32]),
                        op=ALU.mult), reads=[cs, g2, G_], writes=[G_])

            load_x(0)
            load_x(1)
            load_x(2)
            run_chains([(lambda slot, ti=ti: stage1_chain(ti, slot)) for ti in range(4)], 2)
            for B in range(12):
                halo = B < 4
                xb = xnT[B % 2]
                rope_tables(B)
                vs_ = vst[B % 2]
                gs_ = gst[B % 2]
                ks_ = kst[B % 2]
                qs_ = qst[B % 2]
                fm = [("v", 1024, vs_)]
                if not halo:
                    fm += [("gb", 1536, gs_), ("ga", 3072, gaT), ("u", 2048, uT)]
                for (kind, col0, dstT) in fm:
                    for j in range(4):
                        pb = next_pm()
                        for kc in range(8):
                            S.op("pe", lambda e, pb=pb, kc=kc, xb=xb, c0=col0 + j * 128: e.matmul(
                                pb[:, :], lhsT=Wb[:, kc, c0:c0 + 128], rhs=xb[:, kc, :], start=(kc == 0), stop=(kc == 7),
                                skip_group_check=True), reads=[Wb, xb], writes=[pb], sig=(kc == 7))
                        if kind == "gb":
                            src = pb[:, :].rearrange("p (n c) -> p c n", c=4)
                            dsta = dstT[:, j, :].rearrange("p (c n) -> p c n", c=4)
                        else:
                            src = pb[:, :]
                            dsta = dstT[:, j, :]
                        if kind == "v":
                            S.op("act", lambda e, src=src, dsta=dsta: e.activation(out=dsta, in_=src, func=AF.Copy), reads=[pb], writes=[dstT])
                        else:
                            fn = AF.Gelu_apprx_tanh if kind == "u" else AF.Silu
                            S.op("act", lambda e, src=src, dsta=dsta, fn=fn: e.activation(out=dsta, in_=src, func=fn),
                                 reads=[pb], writes=[dstT])
                if not halo:
                    S.op("pool", lambda e: e.tensor_tensor(out=w1[:, :, :], in0=uT[:, :, :], in1=gaT[:, :, :], op=ALU.mult),
                         reads=[uT, gaT], writes=[w1])

                makers = []
                for tl in range(4):
                    makers.append(lambda slot, tl=tl, B=B: qk_chain("k", B, tl, slot))
                    if not halo:
                        makers.append(lambda slot, tl=tl, B=B: qk_chain("q", B, tl, slot))
                        makers.append(lambda slot, tl=tl, B=B: va_chain(B, tl, slot))
                    if B + 1 < 12:
                        makers.append(lambda slot, ti=4 * (B + 1) + tl: stage1_chain(ti, slot))
                run_chains(makers, 5 if halo else NSET)

                c0 = B * 512
                ssem = stsem[B % 2]
                S.group(ssem, 2 if halo else 4)
                S.dma(ssem, lambda e, ks_=ks_, c0=c0: e.dma_start(out=kT_d[:, :, c0:c0 + 512].rearrange("a p l -> p a l"),
                                                                 in_=ks_[:, :, :]), reads=[ks_])
                S.dma(ssem, lambda e, vs_=vs_, c0=c0: e.dma_start(out=vT_d[:, :, c0:c0 + 512].rearrange("a p l -> p a l"),
                                                                 in_=vs_[:, :, :]), reads=[vs_])
                if not halo:
                    o0 = c0 - NHALO
                    S.dma(ssem, lambda e, qs_=qs_, o0=o0: e.dma_start(out=qT_d[:, :, o0:o0 + 512].rearrange("a p l -> p a l"),
                                                                     in_=qs_[:, :, :]), reads=[qs_])
                    S.dma(ssem, lambda e, gs_=gs_, o0=o0: e.dma_start(out=gbT_d[:, :, o0:o0 + 512].rearrange("a p l -> p a l"),
                                                                     in_=gs_[:, :, :]), reads=[gs_])
            S.barrier()
            with nc.Block() as block:
                S.emit(block)

        if stop == "A":
            return nc
        with ExitStack() as BC:
            atT = sb(BC, "atT", [128, 4, NTOK], BF)
            Wo = sb(BC, "Wo", [128, 8, 1024], BF)
            Wg = sb(BC, "Wg", [128, 8, 1024], BF)
            Wp = sb(BC, "Wp", [128, 2, 1024], BF)
            gple = sb(BC, "gple", [128, 8], F32)
            bg_b = sb(BC, "bg_b", [1, 1024], BF)
            ones1 = sb(BC, "ones1", [1, 128], BF)
            ld(gple, gple[:, :], gple_d[:, :])
            S.op("dve", lambda e: e.memset(ones1[:, :], 1.0), writes=[ones1])

            with ExitStack() as Bs:
                wst2 = [sb(Bs, "wst2_0", [128, 1024], F32)] * 2
                ld(wst2[0], wst2[0][0:1, :], bgate_d[:, :])
                S.op("dve", lambda e: e.tensor_copy(bg_b[:, :], wst2[0][0:1, :]), reads=[wst2[0]], writes=[bg_b])
                w2sem = [S.new_dma_sem() for _ in range(2)]
                jobs = [(w_out, Wo, kc, None) for kc in range(8)] + [(w_gate, Wg, kc, gple) for kc in range(8)] + \
                       [(w_pp, Wp, kc, None) for kc in range(2)]
                jcnt = [0]

                def tail_weight_jobs(n):
                    for _ in range(n):
                        if not jobs:
                            return
                        (src, dstW, kc, gsc) = jobs.pop(0)
                        cnt = jcnt[0]
                        jcnt[0] += 1
                        stg = wst2[cnt % 2]
                        S.dma(w2sem[cnt % 2], lambda e, stg=stg, src=src, kc=kc: e.dma_start(out=stg[:, :], in_=src[kc * 128:(kc + 1) * 128, :]),
                              writes=[stg])
                        if gsc is None:
                            S.op("dve", lambda e, stg=stg, dstW=dstW, kc=kc: e.tensor_copy(dstW[:, kc, :], stg[:, :]),
                                 reads=[stg], writes=[dstW])
                        else:
                            S.op("dve", lambda e, stg=stg, dstW=dstW, kc=kc, gsc=gsc: e.tensor_scalar(
                                dstW[:, kc, :], stg[:, :], gsc[:, kc:kc + 1], None, ALU.mult), reads=[stg, gsc], writes=[dstW])

                qs = [sb(Bs, "qs%d" % i, [128, 2048], BF) for i in range(2)]
                ksb = [sb(Bs, "ksb%d" % i, [128, 4096], BF) for i in range(2)]
                vsb = [sb(Bs, "vsb%d" % i, [128, 4096], BF) for i in range(2)]
                gsb = [sb(Bs, "gsb%d" % i, [128, 2048], BF) for i in range(2)]
                Vc = sb(Bs, "Vc", [128, 69, 192], BF)
                LA = 4
                NST = 4
                LAS = 1
                NPT = 6
                pt = [sb(Bs, "pt%d" % i, [128, 512], BF) for i in range(NPT)]
                rec = sb(Bs, "rec", [128, 2048], F32)
                accm = [ps(Bs, "accm%d" % i, [128, 512], F32) for i in range(2)]
                acc16p = ps(Bs, "acc16p", [128, 512], F32)
                a16 = sb(Bs, "a16", [128, 2048], F32)
                stp = [ps(Bs, "stp%d" % i, [128, 512], F32) for i in range(LA + 1)]
                vtp = [TV(stp[i], (lambda i=i: stp[i][:, :].bitcast(BF).rearrange("p (a t) -> p a t", a=8))) for i in range(LA + 1)]
                bsem = [S.new_dma_sem() for _ in range(2)]

                S.op("pool", lambda e: e.memset(Vc[:, :, 64:128], 1.0), writes=[Vc])

                def chunk(ten, rs, kind, sbk, a, b=0):
                    base = ten[rs, sbk * 2048:(sbk + 1) * 2048]
                    if kind == 0:
                        return base.rearrange("p (k c i x) -> p k c i x", k=4, c=4, x=4)[:, :, a % 4, :, a // 4]
                    if kind == 1:
                        return base[:, 512 * b + 128 * a:512 * b + 128 * a + 128]
                    k, q4 = a // 4, a % 4
                    return base[:, 512 * k:512 * k + 512].rearrange("p (c q n) -> p c q n", c=4, q=4)[:, :, q4, :]

                def kchunk(ten, rs, kind, sbk, a, b=0):
                    if kind == 0:
                        st0 = sbk * 2048 + a
                        return ten[rs, st0:st0 + 16 * 127 + 1:16]
                    if kind == 1:
                        st0 = sbk * 2048 + 512 * b + a
                        return ten[rs, st0:st0 + 4 * 127 + 1:4]
                    st0 = sbk * 2048 + 128 * a
                    return ten[rs, st0:st0 + 128]

                def bank_of(kind, a, b):
                    return b if kind == 1 else a // 4

                def accchunk(rs, kind, a, b=0):
                    t_ = accm[bank_of(kind, a, b) % 2]
                    if kind == 1:
                        return t_[rs, 128 * a:128 * a + 128]
                    q4 = a % 4
                    return t_[rs, :].rearrange("p (c q n) -> p c q n", c=4, q=4)[:, :, q4, :]

                blocks = []
                for c in range(16):
                    blocks.append((0, c, 0, c, (0, c, 0, 48 + c)))
                for bk in range(4):
                    for c4 in range(4):
                        bb = bk
                        prev = (0, c4, 3, 64 + c4) if bb == 0 else (1, c4, bb - 1, 16 + c4 * 4 + bb - 1)
                        blocks.append((1, c4, bb, 16 + c4 * 4 + bb, prev))
                    for b1 in range(4 * bk, 4 * bk + 4):
                        prev = (0, 15, 0, 68) if b1 == 0 else (1, b1 - 1, 0, 32 + b1 - 1)
                        blocks.append((2, b1, 0, 32 + b1, prev))
                vlist = []
                for c in range(16):
                    vlist.append((c, 1, 0, c, 0))
                    vlist.append((48 + c, 0, 0, c, 0))
                for c4 in range(4):
                    for bb in range(4):
                        vlist.append((16 + c4 * 4 + bb, 1, 1, c4, bb))
                    vlist.append((64 + c4, 0, 1, c4, 3))
                for b1 in range(16):
                    vlist.append((32 + b1, 1, 2, b1, 0))
                vlist.append((68, 0, 2, 15, 0))
                vlist.sort()

                it = 0
                vtpi = 0
                sti = 0
                pti = 0
                for Sb in (() if skipB else (1, 2)):
                    if Sb == 1:
                        S.op("pool", lambda e: e.tensor_scalar(Vc[:, 48:69, 64:128], Vc[:, 48:69, 64:128], flag[:, 0:1], None, ALU.mult),
                             reads=[flag, Vc], writes=[Vc])
                    else:
                        S.op("pool", lambda e: e.memset(Vc[:, 48:69, 64:128], 1.0), reads=[Vc], writes=[Vc])
                    for p in range(4):
                        bi = it % 2
                        q_, k_, v_, g_ = qs[bi], ksb[bi], vsb[bi], gsb[bi]
                        o0 = (Sb - 1) * 2048
                        S.group(bsem[bi], 4)
                        S.dma(bsem[bi], lambda e, q_=q_, p=p, o0=o0: e.dma_start(out=q_[:, :], in_=qT_d[p, :, o0:o0 + 2048]), writes=[q_])
                        S.dma(bsem[bi], lambda e, k_=k_, p=p, o0=o0: e.dma_start(out=k_[:, :], in_=kT_d[p, :, o0:o0 + 4096]), writes=[k_])
                        S.dma(bsem[bi], lambda e, v_=v_, p=p, o0=o0: e.dma_start(out=v_[:, :], in_=vT_d[p, :, o0:o0 + 4096]), writes=[v_])
                        S.dma(bsem[bi], lambda e, g_=g_, p=p, o0=o0: e.dma_start(out=g_[:, :], in_=gbT_d[p, :, o0:o0 + 2048]), writes=[g_])
                        for i0 in range(0, 69, 8):
                            n = min(8, 69 - i0)
                            tpv = vtp[vtpi % (LA + 1)]
                            vtpi += 1
                            for jx in range(n):
                                (idx, sbk, kind, a, b) = vlist[i0 + jx]
                                assert idx == i0 + jx
                                src = kchunk(v_, slice(0, 128), kind, sbk, a, b)
                                S.op("pe", lambda e, tpv=tpv, jx=jx, src=src: e.transpose(tpv[:, jx, :], src, ident[:, :]),
                                     reads=[v_, ident], writes=[tpv], sig=(jx == n - 1))
                            S.op("dve", lambda e, tpv=tpv, i0=i0, n=n: e.tensor_copy(Vc[:, i0:i0 + n, 0:64], tpv[:, 0:n, 0:64]),
                                 reads=[tpv], writes=[Vc])
                            S.op("act", lambda e, tpv=tpv, i0=i0, n=n: e.activation(out=Vc[:, i0:i0 + n, 128:192], in_=tpv[:, 0:n, 64:128], func=AF.Copy),
                                 reads=[tpv, Vc], writes=[Vc])
                        for hp in range(2):
                            tail_weight_jobs(2)
                            rs = slice(hp * 64, hp * 64 + 64)
                            lo = 0 if hp == 0 else 64
                            first = True
                            pend = []
                            npair = len(blocks) // 2

                            def do_pv(pair, ptile, sig_last, extra_reads=()):
                                for j in range(2):
                                    (kind, a, b, cidx, prev) = blocks[2 * pair + j]
                                    (psb, pa, pb_, pidx) = prev
                                    for half, vidx in ((0, pidx), (1, cidx)):
                                        lw = Vc[:, vidx, lo:lo + 128]
                                        rhs_full = ptile[:, j * 256 + half * 128: j * 256 + half * 128 + 128]
                                        if kind == 0:
                                            o16 = acc16p[:, (a % 4) * 128:(a % 4) * 128 + 128]
                                            S.op("pe", lambda e, lw=lw, o16=o16, rhs_full=rhs_full, half=half: e.matmul(
                                                o16, lhsT=lw, rhs=rhs_full, start=(half == 0), stop=(half == 1), skip_group_check=True),
                                                reads=[Vc, ptile] + list(extra_reads), writes=[acc16p], sig=(sig_last and j == 1 and half == 1))
                                            if half == 1 and a % 4 == 3:
                                                S.op("act", lambda e, a=a: e.activation(out=a16[:, (a - 3) * 128:(a + 1) * 128], in_=acc16p[:, :],
                                                                                      func=AF.Copy), reads=[acc16p], writes=[a16])
                                        else:
                                            oa = accchunk(slice(0, 128), kind, a, b)
                                            bk_ = bank_of(kind, a, b)
                                            am = accm[bk_ % 2]
                                            rr = rhs_full if kind == 1 else rhs_full.rearrange("p (x y) -> p x y", x=4)
                                            st_ = (kind == 1 and a == 0 and half == 0)
                                            S.op("pe", lambda e, lw=lw, oa=oa, rr=rr, st_=st_: e.matmul(oa, lhsT=lw, rhs=rr, start=st_, stop=False,
                                                                                                      skip_group_check=True),
                                                 reads=[Vc, ptile] + list(extra_reads), writes=[am], sig=(sig_last and j == 1 and half == 1))
                                            if kind == 2 and a % 4 == 3 and half == 1:
                                                S.op("act", lambda e, am=am, bk_=bk_: e.activation(out=rec[:, 512 * bk_:512 * bk_ + 512], in_=am[:, :],
                                                                                                  func=AF.Copy), reads=[am], writes=[rec])

                            for ss in range(npair // 2):
                                stbs = []
                                for pi_ in range(2):
                                    pair = 2 * ss + pi_
                                    stb = stp[sti % NST]
                                    sti += 1
                                    stbs.append(stb)
                                    for j in range(2):
                                        (kind, a, b, cidx, prev) = blocks[2 * pair + j]
                                        (psb, pa, pb_, pidx) = prev
                                        qa = chunk(q_, rs, kind, 0, a, b)
                                        kprev = kchunk(k_, rs, kind, psb, pa, pb_)
                                        kcur = kchunk(k_, rs, kind, 1, a, b)
                                        S.op("pe", lambda e, stb=stb, j=j, kprev=kprev, qa=qa: e.matmul(
                                            stb[:, j * 256:j * 256 + 128], lhsT=kprev, rhs=qa, start=True, stop=True, skip_group_check=True),
                                            reads=[k_, q_], writes=[stb], sig=False)
                                        last = (pi_ == 1 and j == 1)
                                        S.op("pe", lambda e, stb=stb, j=j, kcur=kcur, qa=qa: e.matmul(
                                            stb[:, j * 256 + 128:j * 256 + 256], lhsT=kcur, rhs=qa, start=True, stop=True, skip_group_check=True),
                                            reads=[k_, q_], writes=(list(stbs) if last else [stb]), sig=last)
                                ptiles = []
                                for pi_ in range(2):
                                    pair = 2 * ss + pi_
                                    stb = stbs[pi_]
                                    ptile = pt[pti % NPT]
                                    pti += 1
                                    ptiles.append(ptile)
                                    S.op("act", lambda e, stb=stb, ptile=ptile: e.activation(out=ptile[:, :], in_=stb[:, :], func=AF.Exp, scale=0.125),
                                         reads=[stb], writes=[ptile])
                                    kind = blocks[2 * pair][0]
                                    mk = masks[:, 2 * kind:2 * kind + 2, :].unsqueeze(1).broadcast_to([128, 2, 2, 128])
                                    pv4 = ptile[:, :].rearrange("p (j h n) -> p j h n", j=2, h=2)
                                    meng = "dve"
                                    S.op(meng, lambda e, pv4=pv4, mk=mk: e.tensor_tensor(out=pv4, in0=pv4, in1=mk, op=ALU.mult),
                                         reads=[ptile, masks], writes=[ptile])
                                pend.append((ss, ptiles))
                                if len(pend) > LAS:
                                    ss0, pts = pend.pop(0)
                                    do_pv(2 * ss0, pts[0], False)
                                    do_pv(2 * ss0 + 1, pts[1], True, extra_reads=[pts[0]])
                            while pend:
                                ss0, pts = pend.pop(0)
                                do_pv(2 * ss0, pts[0], False)
                                do_pv(2 * ss0 + 1, pts[1], True, extra_reads=[pts[0]])
                            num = slice(0, 64) if hp == 0 else slice(64, 128)
                            den = slice(64, 128) if hp == 0 else slice(0, 64)
                            a16v = a16[:, :].rearrange("p (cc c k i) -> p k c i cc", cc=4, c=4, k=4)
                            recv = rec[:, :].rearrange("p (k c i cc) -> p k c i cc", k=4, c=4, cc=4)
                            for kk in range(4):
                                S.op("dve", lambda e, kk=kk: e.tensor_tensor(out=recv[:, kk], in0=recv[:, kk], in1=a16v[:, kk], op=ALU.add),
                                     reads=[rec, a16], writes=[rec])
                            S.op("act", lambda e, num=num, den=den: e.activation(out=a16[num, :], in_=rec[den, :], func=AF.Ln), reads=[rec], writes=[a16])
                            S.op("act", lambda e, num=num: e.activation(out=a16[num, :], in_=a16[num, :], func=AF.Exp, scale=-1.0), reads=[a16], writes=[a16])
                            S.op("dve", lambda e, num=num: e.tensor_tensor(out=rec[num, :], in0=rec[num, :], in1=a16[num, :], op=ALU.mult),
                                 reads=[rec, a16], writes=[rec])
                            S.op("pool", lambda e, num=num, g_=g_, p=p, o0=o0: e.tensor_tensor(out=atT[num, p, o0:o0 + 2048], in0=rec[num, :],
                                                                                          in1=g_[num, :], op=ALU.mult),
                                 reads=[rec, g_], writes=[atT])
                        it += 1
                if dbg:
                    at_dbg = nc.dram_tensor("at_dbg", [128, 4, NTOK], BF, kind="ExternalOutput").ap()
                    gm_dbg = nc.dram_tensor("gm_dbg", [128, 4, NTOK], BF, kind="ExternalOutput").ap()
                    S.dma(dsem_c, lambda e: e.dma_start(out=at_dbg[:, :, :], in_=atT[:, :, :]), reads=[atT])
                    S.dma(dsem_c, lambda e: e.dma_start(out=gm_dbg[:, :, :], in_=gmT[:, :, :]), reads=[gmT])
                S.barrier()
                with nc.Block() as block:
                    S.emit(block)

            if stop == "B":
                return nc
            with ExitStack() as Cs:
                NS = 2
                xt2 = [sb(Cs, "xt2_%d" % i, [128, 1024], F32) for i in range(NS)]
                pt2 = [sb(Cs, "pt2_%d" % i, [128, 256], F32) for i in range(NS)]
                pbf = [sb(Cs, "pbf%d" % i, [128, 256], BF) for i in range(NS)]
                pT = [sb(Cs, "pT%d" % i, [128, 2, 128], BF) for i in range(NS)]
                h_ = [sb(Cs, "h%d" % i, [128, 1024], F32) for i in range(NS)]
                hsq = sb(Cs, "hsq", [128, 1024], BF)
                hst = [sb(Cs, "hst%d" % i, [128, 4], F32) for i in range(NS)]
                hn = [sb(Cs, "hn%d" % i, [128, 1024], BF) for i in range(NS)]
                hnT = [sb(Cs, "hnT%d" % i, [128, 8, 128], BF) for i in range(NS)]
                gate = [sb(Cs, "gate%d" % i, [128, 1024], F32) for i in range(NS)]
                ot = [sb(Cs, "ot%d" % i, [128, 1024], F32) for i in range(NS)]
                hp_ = [[ps(Cs, "hp%d_%d" % (i, g), [128, 512], F32) for g in range(2)] for i in range(NS)]
                gp_ = [[ps(Cs, "gp%d_%d" % (i, g), [128, 512], F32) for g in range(2)] for i in range(NS)]
                tpv8 = [[TV(gp_[i][g], (lambda i=i, g=g: gp_[i][g][:, :].bitcast(BF).rearrange("p (a t) -> p a t", a=8)))
                         for g in range(2)] for i in range(NS)]
                csem = [S.new_dma_sem() for _ in range(NS)]
                osem = [S.new_dma_sem() for _ in range(NS)]

                tiles = [(Sb, k, w) for Sb in (0, 1) for k in range(4) for w in range(4)]

                def rows(Sb, k, w):
                    return Sb * 2048 + 512 * k + w

                def load_c(i, slot):
                    (Sb, k, w) = tiles[i]
                    x_ = xt2[slot]
                    p_ = pt2[slot]
                    S.group(csem[slot], 2)
                    r0 = rows(Sb, k, w)
                    S.dma(csem[slot], lambda e, x_=x_, r0=r0: e.dma_start(
                        out=x_[:, :], in_=xh[NHALO + r0:NHALO + r0 + 509:4, :]), writes=[x_])
                    S.dma(csem[slot], lambda e, p_=p_, r0=r0: e.dma_start(
                        out=p_[:, :], in_=pown[r0:r0 + 509:4, :]), writes=[p_])

                def tail_chain(i, slot):
                    (Sb, k, w) = tiles[i]
                    L0 = Sb * 2048 + 512 * k + 128 * w
                    x_, p_, hh, hs, h_n, h_nT = xt2[slot], pt2[slot], h_[slot], hst[slot], hn[slot], hnT[slot]
                    pb_, p_T, gt, oo = pbf[slot], pT[slot], gate[slot], ot[slot]
                    hp, gp, tp = hp_[slot], gp_[slot], tpv8[slot]

                    def t0():
                        for g in range(2):
                            for e8 in range(8):
                                srcT = atT if e8 < 4 else gmT
                                S.op("pe", lambda e, g=g, e8=e8, srcT=srcT: e.matmul(
                                    hp[g][:, :], lhsT=srcT[:, e8 % 4, L0:L0 + 128], rhs=Wo[:, e8, g * 512:(g + 1) * 512],
                                    start=(e8 == 0), stop=(e8 == 7), skip_group_check=True),
                                    reads=[atT, gmT, Wo], writes=([hp[0], hp[1]] if g == 1 else [hp[g]]), sig=(g == 1 and e8 == 7))

                    def t1():
                        for g in range(2):
                            S.op("dve", lambda e, g=g: e.tensor_tensor(out=hh[:, g * 512:(g + 1) * 512], in0=hp[g][:, :],
                                                                    in1=x_[:, g * 512:(g + 1) * 512], op=ALU.add),
                                 reads=[hp[g], x_, hh], writes=[hh])
                        S.op("pool", lambda e: e.memset(hs[:, :], 0.0), writes=[hs])
                        S.op("pool", lambda e: e.tensor_copy(pb_[:, :], p_[:, :]), reads=[p_], writes=[pb_])
                        if i + NS < len(tiles):
                            load_c(i + NS, slot)

                    def t2():
                        S.op("act", lambda e: e.activation(out=hsq[:, :], in_=hh[:, :], func=AF.Square, accum_out=hs[:, 0:1]),
                             reads=[hh, hs], writes=[hsq, hs])
                        for kc in range(2):
                            S.op("pe", lambda e, kc=kc: e.transpose(tp[1][:, kc, :], pb_[:, kc * 128:(kc + 1) * 128], ident[:, :]),
                                 reads=[pb_, ident], writes=[tp[1]], sig=(kc == 1))

                    def t3():
                        S.op("dve", lambda e: e.tensor_scalar(hs[:, 1:2], hs[:, 0:1], 1.0 / 1024, EPS, ALU.mult, ALU.add), reads=[hs], writes=[hs])
                        S.op("act", lambda e: e.activation(out=p_T[:, :, :], in_=tp[1][:, 0:2, :], func=AF.Copy), reads=[tp[1]], writes=[p_T])

                    def t4():
                        S.op("pool", lambda e: e.tensor_tensor(out=hs[:, 2:3], in0=hs[:, 1:2], in1=mhalf[:, 0:1], op=ALU.pow),
                             reads=[hs, mhalf], writes=[hs])

                    def t5():
                        S.op("act", lambda e: e.activation(out=h_n[:, :], in_=hh[:, :], func=AF.Copy, scale=hs[:, 2:3]), reads=[hh, hs], writes=[h_n])

                    def t6():
                        for kc in range(8):
                            S.op("pe", lambda e, kc=kc: e.transpose(tp[0][:, kc, :], h_n[:, kc * 128:(kc + 1) * 128], ident[:, :]),
                                 reads=[h_n, ident], writes=[tp[0]], sig=(kc == 7))

                    def t7():
                        S.op("dve", lambda e: e.tensor_copy(h_nT[:, :, :], tp[0][:, :, :]), reads=[tp[0]], writes=[h_nT])

                    def t8():
                        for g in range(2):
                            for kc in range(8):
                                S.op("pe", lambda e, g=g, kc=kc: e.matmul(gp[g][:, :], lhsT=h_nT[:, kc, :], rhs=Wg[:, kc, g * 512:(g + 1) * 512],
                                                                           start=(kc == 0), stop=False, skip_group_check=True),
                                     reads=[h_nT, Wg], writes=[gp[g]], sig=False)
                            S.op("pe", lambda e, g=g: e.matmul(gp[g][:, :], lhsT=ones1[0:1, :], rhs=bg_b[0:1, g * 512:(g + 1) * 512],
                                                               start=False, stop=True, skip_group_check=True),
                                 reads=[h_nT, Wg, ones1, bg_b], writes=([gp[0], gp[1]] if g == 1 else [gp[g]]), sig=(g == 1))
                        for g in range(2):
                            for kc in range(2):
                                S.op("pe", lambda e, g=g, kc=kc: e.matmul(hp[g][:, :], lhsT=p_T[:, kc, :], rhs=Wp[:, kc, g * 512:(g + 1) * 512],
                                                                           start=(kc == 0), stop=(kc == 1), skip_group_check=True),
                                     reads=[p_T, Wp], writes=([hp[0], hp[1]] if g == 1 else [hp[g]]), sig=(g == 1 and kc == 1))

                    def t9():
                        for g in range(2):
                            S.op("act", lambda e, g=g: e.activation(out=gt[:, g * 512:(g + 1) * 512], in_=gp[g][:, :], func=AF.Sigmoid),
                                 reads=[gp[g], gt], writes=[gt])

                    def t10():
                        for g in range(2):
                            S.op("dve", lambda e, g=g: e.tensor_tensor(out=oo[:, g * 512:(g + 1) * 512], in0=hp[g][:, :],
                                                                      in1=gt[:, g * 512:(g + 1) * 512], op=ALU.mult),
                                 reads=[hp[g], gt, oo], writes=[oo])

                    def t11():
                        S.op("pool", lambda e: e.tensor_tensor(out=oo[:, :], in0=oo[:, :], in1=hh[:, :], op=ALU.add), reads=[oo, hh], writes=[oo])
                        S.group(osem[slot], 1)
                        r0 = rows(Sb, k, w)
                        S.dma(osem[slot], lambda e, r0=r0: e.dma_start(out=out[r0:r0 + 509:4, :], in_=oo[:, :]), reads=[oo])

                    return [t0, t1, t2, t3, t4, t5, t6, t7, t8, t9, t10, t11]

                for i0 in range(NS):
                    load_c(i0, i0)
                run_chains_g([(lambda slot, i=i: tail_chain(i, slot)) for i in range(len(tiles))], NS)
                S.barrier()
                with nc.Block() as block:
                    S.emit(block)
    return nc


_CACHE = {}


def kernel(x, p, positions, g_in, w_in, q_norm, k_norm, w_spatial, b_spatial,
           ln_v_g, ln_v_b, w_out, g_ple, w_ple_gate, b_ple_gate, w_ple_proj):
    f32 = np.float32
    x = np.asarray(x, f32)
    p = np.asarray(p, f32)[0]
    positions = np.asarray(positions, np.int32)
    ident, masks, invf, tril = _const_tables()
    if "nc" not in _CACHE:
        _CACHE["nc"] = build_nc()
    nc = _CACHE["nc"]

    def col8(v):
        return np.ascontiguousarray(np.asarray(v, f32).reshape(8, 128).T)

    shared = {
        "w_in": np.ascontiguousarray(np.asarray(w_in, f32)[0]),
        "w_out": np.ascontiguousarray(np.asarray(w_out, f32)[0]),
        "w_gate": np.ascontiguousarray(np.asarray(w_ple_gate, f32)[0]),
        "w_pp": np.ascontiguousarray(np.asarray(w_ple_proj, f32)[0]),
        "gin": col8(np.asarray(g_in)[0]),
        "gple": col8(np.asarray(g_ple)[0]),
        "qn_b": np.ascontiguousarray(np.broadcast_to(np.asarray(q_norm, f32)[0][None, :], (128, 64))),
        "kn_b": np.ascontiguousarray(np.broadcast_to(np.asarray(k_norm, f32)[0][None, :], (128, 64))),
        "lng": np.ascontiguousarray(np.asarray(ln_v_g, f32)[0].reshape(4, 128).T),
        "lnb": np.ascontiguousarray(np.asarray(ln_v_b, f32)[0].reshape(4, 128).T),
        "bsb": np.ascontiguousarray(np.broadcast_to(np.asarray(b_spatial, f32)[0].reshape(1, 512), (128, 512))),
        "wsT": np.ascontiguousarray(np.asarray(w_spatial, f32)[0].transpose(2, 0, 1).reshape(128, 512)),
        "bgate": np.ascontiguousarray(np.asarray(b_ple_gate, f32)[0].reshape(1, 1024)),
        "ident": ident, "masks": np.ascontiguousarray(masks.reshape(128, 768)), "invf": invf, "tril": tril,
    }
    in_maps = []
    for core in range(8):
        b, half = core // 2, core % 2
        s0 = half * NTOK
        if half == 0:
            xhalo = np.zeros((NHALO, 1024), f32)
            phalo = np.zeros((NHALO,), np.int32)
        else:
            xhalo = x[b, s0 - NHALO:s0]
            phalo = positions[b, s0 - NHALO:s0]
        xh = np.concatenate([xhalo, x[b, s0:s0 + NTOK]], axis=0)
        pos = np.concatenate([phalo, positions[b, s0:s0 + NTOK]], axis=0)
        m = dict(shared)
        m["xh"] = np.ascontiguousarray(xh)
        m["pown"] = np.ascontiguousarray(p[b, s0:s0 + NTOK])
        m["pos_t"] = np.ascontiguousarray(pos.reshape(48, 128).T)
        m["flag"] = np.full((128, 1), float(half), f32)
        in_maps.append(m)
    res = run_bass_kernel_spmd(nc, in_maps, core_ids=list(range(8)))
    outp = np.empty((4, 8192, 1024), f32)
    for core in range(8):
        b, half = core // 2, core % 2
        outp[b, half * NTOK:(half + 1) * NTOK] = res.results[core]["out"]
    return outp
```

```python
import math
from contextlib import ExitStack

import numpy as np
import ml_dtypes

import concourse.bass as bass
import concourse.mybir as mybir
from concourse.bass_utils import run_bass_kernel_spmd

F32 = mybir.dt.float32
BF = mybir.dt.bfloat16
I32 = mybir.dt.int32
AF = mybir.ActivationFunctionType
ALU = mybir.AluOpType
AX = mybir.AxisListType

ENGS = ("pe", "act", "dve", "pool", "sp")
EPS = 1e-6
NTOK = 4096
NHALO = 2048
NALL = NTOK + NHALO
PI = math.pi


class Buf:
    __slots__ = ("w", "r")

    def __init__(self):
        self.w = None
        self.r = []


class T:
    def __init__(self, t):
        self.t = t
        self.b = Buf()

    def __getitem__(self, k):
        return self.t[k]


class TV(T):
    def __init__(self, base, fn):
        self.b = base.b
        self.fn = fn

    def __getitem__(self, k):
        return self.fn()[k]


class Sched:
    def __init__(self, nc):
        self.nc = nc
        self.ops = {e: [] for e in ENGS}
        self.cnt = {}
        self.sems = {}
        self.seen = {e: {} for e in ENGS}
        self.dma_free = []
        self.gtarget = {}

    def open(self, stack, n_dma_sems):
        for e in ENGS:
            self.sems[e] = stack.enter_context(self.nc.semaphore("s_" + e))
            self.cnt[e] = 0
        for i in range(n_dma_sems):
            nm = "d%d" % i
            self.sems[nm] = stack.enter_context(self.nc.semaphore("s_" + nm))
            self.cnt[nm] = 0
            self.dma_free.append(nm)

    def new_dma_sem(self):
        return self.dma_free.pop(0)

    def _deps(self, eng, reads, writes):
        deps = {}
        for b in reads:
            if b.w is not None:
                s, v = b.w
                deps[s] = max(deps.get(s, 0), v)
        for b in writes:
            if b.w is not None:
                s, v = b.w
                deps[s] = max(deps.get(s, 0), v)
            for (s, v) in b.r:
                deps[s] = max(deps.get(s, 0), v)
        waits = []
        for s, v in deps.items():
            if self.seen[eng].get(s, 0) >= v:
                continue
            self.seen[eng][s] = v
            waits.append((s, v))
        return waits

    def _reg(self, tok, reads, writes):
        for b in reads:
            b.r.append(tok)
            if len(b.r) > 64:
                best = {}
                for s, v in b.r:
                    best[s] = max(best.get(s, 0), v)
                b.r = list(best.items())
        for b in writes:
            b.w = tok
            b.r = []

    def op(self, eng, fn, reads=(), writes=(), sig=True):
        reads = [x.b if isinstance(x, T) else x for x in reads]
        writes = [x.b if isinstance(x, T) else x for x in writes]
        waits = self._deps(eng, reads, writes)
        if sig:
            self.cnt[eng] += 1
            self._reg((eng, self.cnt[eng]), reads, writes)
        self.ops[eng].append((waits, fn, (eng, 1) if sig else None))

    def group(self, sem, n):
        self.gtarget[sem] = self.cnt[sem] + 16 * n

    def dma(self, sem, fn, reads=(), writes=(), eng="sp"):
        reads = [x.b if isinstance(x, T) else x for x in reads]
        writes = [x.b if isinstance(x, T) else x for x in writes]
        tgt = self.gtarget.get(sem, 0)
        saved = self.seen[eng].get(sem, 0)
        if tgt > self.cnt[sem]:
            self.seen[eng][sem] = max(saved, tgt)
        waits = self._deps(eng, reads, writes)
        self.seen[eng][sem] = max(saved, max([v for (s_, v) in waits if s_ == sem] + [saved]))
        self.cnt[sem] += 16
        tokv = tgt if tgt >= self.cnt[sem] else self.cnt[sem]
        self._reg((sem, tokv), reads, writes)
        self.ops[eng].append((waits, fn, (sem, 16)))

    def wait_all(self, eng):
        waits = []
        for s, v in self.cnt.items():
            if s == eng:
                continue
            if v > 0 and self.seen[eng].get(s, 0) < v:
                self.seen[eng][s] = v
                waits.append((s, v))
        self.ops[eng].append((waits, None, None))

    def barrier(self):
        for e in ENGS:
            self.wait_all(e)

    def emit(self, block):
        def run(ename):
            ops = self.ops[ename]

            def body(h):
                for waits, fn, inc in ops:
                    for s, v in waits:
                        h.wait_ge(self.sems[s], v)
                    if fn is None:
                        continue
                    ins = fn(h)
                    if inc is not None:
                        ins.then_inc(self.sems[inc[0]], inc[1])
            return body

        block.tensor(run("pe"))
        block.scalar(run("act"))
        block.vector(run("dve"))
        block.gpsimd(run("pool"))
        block.sync(run("sp"))
        self.ops = {e: [] for e in ENGS}


def run_chains_g(makers, W):
    pending = list(makers)
    active = []
    free = list(range(W))
    while pending or active:
        while pending and free:
            slot = free.pop(0)
            active.append([slot, pending.pop(0)(slot), 0])
        for a in list(active):
            a[1][a[2]]()
            a[2] += 1
            if a[2] == len(a[1]):
                active.remove(a)
                free.append(a[0])


def _const_tables():
    ident = np.eye(128, dtype=np.float32).astype(ml_dtypes.bfloat16)
    n = np.arange(128)
    pos16 = n
    pos4 = n
    pos1 = 4 * (n % 32) + n // 32
    masks = np.zeros((128, 6, 128), np.float32)
    for br, pos in enumerate((pos16, pos4, pos1)):
        pk = n[:, None]
        pq = pos[None, :]
        masks[:, 2 * br + 0, :] = (pk >= pq)
        masks[:, 2 * br + 1, :] = (pk <= pq)
    masks = masks.astype(ml_dtypes.bfloat16)
    half = 32
    invf = np.exp(-math.log(10000.0) * np.arange(half, dtype=np.float32) / half).astype(np.float32)
    invf = np.broadcast_to(invf[None, :], (128, 32)).copy()
    tril = (n[:, None] <= n[None, :]).astype(np.float32)
    return ident, masks, invf, tril


def build_nc(stop=None, dbg=False, skipB=False):
    nc = bass.Bass("TRN2", target_bir_lowering=False)
    sk = "ExternalOutput"

    def din(name, shape, dt=F32):
        return nc.dram_tensor(name, shape, dt, kind="ExternalInput").ap()

    xh = din("xh", [NALL, 1024])
    pown = din("pown", [NTOK, 256])
    pos_t = din("pos_t", [128, 48], I32)
    flag_d = din("flag", [128, 1])
    w_in = din("w_in", [1024, 3584])
    w_out = din("w_out", [1024, 1024])
    w_gate = din("w_gate", [1024, 1024])
    w_pp = din("w_pp", [256, 1024])
    gin_d = din("gin", [128, 8])
    gple_d = din("gple", [128, 8])
    qn_d = din("qn_b", [128, 64])
    kn_d = din("kn_b", [128, 64])
    lng_d = din("lng", [128, 4])
    lnb_d = din("lnb", [128, 4])
    bsb_d = din("bsb", [128, 512])
    wsT_d = din("wsT", [128, 512])
    bgate_d = din("bgate", [1, 1024])
    ident_d = din("ident", [128, 128], BF)
    masks_d = din("masks", [128, 768], BF)
    invf_d = din("invf", [128, 32])
    tril_d = din("tril", [128, 128])
    out = nc.dram_tensor("out", [NTOK, 1024], F32, kind="ExternalOutput").ap()

    kT_d = nc.dram_tensor("kT_d", [4, 128, NALL], BF, kind=sk).ap()
    vT_d = nc.dram_tensor("vT_d", [4, 128, NALL], BF, kind=sk).ap()
    qT_d = nc.dram_tensor("qT_d", [4, 128, NTOK], BF, kind=sk).ap()
    gbT_d = nc.dram_tensor("gbT_d", [4, 128, NTOK], BF, kind=sk).ap()

    with ExitStack() as G:
        S = Sched(nc)
        S.open(G, 64)

        def sb(stack, name, shape, dt):
            return T(stack.enter_context(nc.sbuf_tensor("sb_" + name, shape, dt)))

        def ps(stack, name, shape, dt):
            return T(stack.enter_context(nc.psum_tensor("ps_" + name, shape, dt)))

        ident = sb(G, "ident", [128, 128], BF)
        masks = sb(G, "masks", [128, 6, 128], BF)
        flag = sb(G, "flag", [128, 1], F32)
        gmT = sb(G, "gmT", [128, 4, NTOK], BF)
        ones_bf = sb(G, "ones_bf", [128, 128], BF)
        dsem_c = S.new_dma_sem()

        def ld(dst, dst_ap, src_ap, sem=None):
            S.dma(sem or S.new_dma_sem(), lambda e: e.dma_start(out=dst_ap, in_=src_ap), writes=[dst])

        ld(ident, ident[:, :], ident_d[:, :])
        ld(masks, masks[:, :, :], masks_d[:, :].rearrange("p (a b) -> p a b", a=6))
        ld(flag, flag[:, :], flag_d[:, :])
        S.op("pool", lambda e: e.memset(ones_bf[:, :], 1.0), writes=[ones_bf])
        mhalf = sb(G, "mhalf", [128, 8], F32)
        S.op("pool", lambda e: e.memset(mhalf[:, :], -0.5), writes=[mhalf])

        with ExitStack() as A:
            Wb = sb(A, "Wb", [128, 8, 3584], BF)
            gin = sb(A, "gin", [128, 8], F32)
            qnb = sb(A, "qnb", [128, 64], F32)
            knb = sb(A, "knb", [128, 64], F32)
            gq1 = sb(A, "gq1", [128, 64], F32)
            gq2 = sb(A, "gq2", [128, 64], F32)
            gk1 = sb(A, "gk1", [128, 64], F32)
            gk2 = sb(A, "gk2", [128, 64], F32)
            lng = sb(A, "lng", [128, 4], F32)
            lnb = sb(A, "lnb", [128, 4], F32)
            wsb = sb(A, "wsb", [128, 4, 128], BF)
            tril = sb(A, "tril", [128, 128], F32)
            biasT = sb(A, "biasT", [128, 4, 128], F32)
            invf = sb(A, "invf", [128, 32], F32)
            posi = sb(A, "posi", [128, 48], I32)
            posf = sb(A, "posf", [128, 48], F32)

            xt = [sb(A, "xt%d" % i, [128, 1024], F32) for i in range(3)]
            xsq = sb(A, "xsq", [128, 1024], BF)
            xn = [sb(A, "xn%d" % i, [128, 1024], BF) for i in range(2)]
            xnT = [sb(A, "xnT%d" % i, [128, 8, 512], BF) for i in range(2)]
            st1 = [sb(A, "st1_%d" % i, [128, 4], F32) for i in range(3)]
            ang = sb(A, "ang", [128, 4, 32], F32)
            ni = sb(A, "ni", [128, 4, 32], I32)
            nf = sb(A, "nf", [128, 4, 32], F32)
            rr_ = sb(A, "rr_", [128, 4, 32], F32)
            arg = sb(A, "arg", [128, 2, 4, 32], F32)
            cs = sb(A, "cs", [128, 2, 4, 32], F32)
            Gq = sb(A, "Gq", [128, 2, 4, 64], F32)
            Gk = sb(A, "Gk", [128, 2, 4, 64], F32)
            NSET = 6
            sets = []
            for i in range(NSET):
                sets.append(dict(f0=sb(A, "f0_%d" % i, [128, 512], F32), f1=sb(A, "f1_%d" % i, [128, 512], F32),
                                 f2=sb(A, "f2_%d" % i, [128, 512], F32), b0=sb(A, "b0_%d" % i, [128, 512], BF),
                                 s8a=sb(A, "s8a_%d" % i, [128, 8], F32), s8b=sb(A, "s8b_%d" % i, [128, 8], F32),
                                 yst=sb(A, "yst_%d" % i, [128, 4], F32)))
            bsb = TV(sets[0]["f0"], lambda: sets[0]["f0"][:, :].rearrange("p (g t) -> p g t", g=4))
            wsT = TV(sets[0]["f1"], lambda: sets[0]["f1"][:, :].rearrange("p (g t) -> p g t", g=4))
            uT = sb(A, "uT", [128, 4, 512], BF)
            gaT = sb(A, "gaT", [128, 4, 512], BF)
            w1 = uT
            wst = [TV(uT, lambda: uT[:, :, :].rearrange("p a l -> p (a l)").bitcast(F32)[:, 0:896]),
                   TV(gaT, lambda: gaT[:, :, :].rearrange("p a l -> p (a l)").bitcast(F32)[:, 0:896])]
            kst = [sb(A, "kst0", [128, 4, 512], BF)] * 2
            qst = [sb(A, "qst0", [128, 4, 512], BF)] * 2
            vst = [sb(A, "vst0", [128, 4, 512], BF)] * 2
            gst = [sb(A, "gst0", [128, 4, 512], BF)] * 2

            tpx = [ps(A, "tpx%d" % i, [128, 8, 128], BF) for i in range(2)]
            pm = [ps(A, "pm%d" % i, [128, 512], F32) for i in range(6)]
            pmain = pm
            paux = pm
            tpq = [TV(pm[i], (lambda i=i: pm[i][:, :].bitcast(BF).rearrange("p (a t) -> p a t", a=8))) for i in range(6)]

            for (dst, src) in ((gin, gin_d), (qnb, qn_d), (knb, kn_d), (lng, lng_d), (lnb, lnb_d),
                               (tril, tril_d), (invf, invf_d), (posi, pos_t)):
                ld(dst, dst[:, :], src[:, :])
            ld(bsb, bsb[:, :, :], bsb_d[:, :].rearrange("p (g t) -> p g t", g=4))
            ld(wsT, wsT[:, :, :], wsT_d[:, :].rearrange("p (g t) -> p g t", g=4))
            S.op("dve", lambda e: e.tensor_copy(posf[:, :], posi[:, :]), reads=[posi], writes=[posf])
            for (src, g1, g2) in ((qnb, gq1, gq2), (knb, gk1, gk2)):
                S.op("dve", lambda e, src=src, g1=g1: e.tensor_scalar(g1[:, :], src[:, :], 1.0, None, ALU.mult),
                     reads=[src], writes=[g1])
                S.op("dve", lambda e, src=src, g2=g2: e.tensor_scalar(g2[:, 0:32], src[:, 32:64], -1.0, None, ALU.mult),
                     reads=[src], writes=[g2])
                S.op("dve", lambda e, src=src, g2=g2: e.tensor_scalar(g2[:, 32:64], src[:, 0:32], 1.0, None, ALU.mult),
                     reads=[src, g2], writes=[g2])
            S.op("dve", lambda e: e.tensor_tensor(out=wsb[:, :, :], in0=wsT[:, :, :],
                                                  in1=tril[:, :].unsqueeze(1).broadcast_to([128, 4, 128]), op=ALU.mult),
                 reads=[wsT, tril], writes=[wsb])
            for g in range(4):
                S.op("pe", lambda e, g=g: e.matmul(pm[0][:, g * 128:(g + 1) * 128], lhsT=ones_bf[:, :], rhs=wsb[:, g, :],
                                                   start=True, stop=True, skip_group_check=True),
                     reads=[ones_bf, wsb], writes=[pm[0]])
            for g in range(4):
                S.op("dve", lambda e, g=g: e.scalar_tensor_tensor(out=biasT[:, g, :], in0=pm[0][:, g * 128:(g + 1) * 128],
                                                                  scalar=lnb[:, g:g + 1], in1=bsb[:, g, :],
                                                                  op0=ALU.mult, op1=ALU.add),
                     reads=[pm[0], lnb, bsb, biasT], writes=[biasT])

            wsem = [S.new_dma_sem() for _ in range(2)]
            cnt = 0
            for half in range(4):
                for kc in range(8):
                    stg = wst[cnt % 2]
                    S.dma(wsem[cnt % 2], lambda e, stg=stg, kc=kc, half=half: e.dma_start(
                        out=stg[:, :], in_=w_in[kc * 128:(kc + 1) * 128, half * 896:(half + 1) * 896]), writes=[stg])
                    if cnt % 2 == 0:
                        S.op("dve", lambda e, stg=stg, kc=kc, half=half: e.tensor_scalar(
                            Wb[:, kc, half * 896:(half + 1) * 896], stg[:, :], gin[:, kc:kc + 1], None, ALU.mult),
                            reads=[stg, gin], writes=[Wb])
                    else:
                        S.op("act", lambda e, stg=stg, kc=kc, half=half: e.activation(
                            out=Wb[:, kc, half * 896:(half + 1) * 896], in_=stg[:, :], func=AF.Copy,
                            scale=gin[:, kc:kc + 1]), reads=[stg, gin], writes=[Wb])
                    cnt += 1

            xsem = [S.new_dma_sem() for _ in range(3)]
            stsem = [S.new_dma_sem() for _ in range(2)]
            pmi = [0]

            def next_pm():
                b = pm[pmi[0] % 6]
                pmi[0] += 1
                return b

            def load_x(ti):
                d = xt[ti % 3]
                S.dma(xsem[ti % 3], lambda e, d=d, ti=ti: e.dma_start(out=d[:, :], in_=xh[ti * 128:(ti + 1) * 128, :]),
                      writes=[d])

            def rr(x):
                return x[:, :].rearrange("p (h d) -> p h d", h=8)

            def qk_chain(kind, B, tl, slot):
                sc = sets[slot]
                f0, f1, f2, b0, s8a, s8b = sc["f0"], sc["f1"], sc["f2"], sc["b0"], sc["s8a"], sc["s8b"]
                xb = xnT[B % 2]
                col0 = 512 if kind == "k" else 0
                G_ = Gk if kind == "k" else Gq
                dst = (kst if kind == "k" else qst)[B % 2]
                tpb = tpq[slot]
                permute = kind == "q"
                st = {}

                def t0():
                    pb = st["pb"] = pmain[slot]
                    for kc in range(8):
                        S.op("pe", lambda e, kc=kc: e.matmul(pb[:, :], lhsT=xb[:, kc, tl * 128:(tl + 1) * 128], rhs=Wb[:, kc, col0:col0 + 512],
                                                             start=(kc == 0), stop=(kc == 7), skip_group_check=True),
                             reads=[Wb, xb], writes=[pb], sig=(kc == 7))

                def t1_():
                    pb = st["pb"]
                    S.op("act", lambda e: e.activation(out=f0[:, :], in_=pb[:, :], func=AF.Square), reads=[pb], writes=[f0])

                def t2_():
                    S.op("dve", lambda e: e.reduce_sum(out=s8a[:, :], in_=rr(f0), axis=AX.X), reads=[f0], writes=[s8a])
                    S.op("dve", lambda e: e.tensor_scalar(s8b[:, :], s8a[:, :], 1.0 / 64, EPS, ALU.mult, ALU.add), reads=[s8a], writes=[s8b])

                def t3_():
                    S.op("pool", lambda e: e.tensor_tensor(out=s8b[:, :], in0=s8b[:, :], in1=mhalf[:, :], op=ALU.pow),
                         reads=[s8b, mhalf], writes=[s8b])

                def t4_():
                    pb = st["pb"]
                    S.op("dve", lambda e: e.tensor_tensor(out=rr(f1), in0=rr(pb), in1=s8b[:, :].unsqueeze(2).broadcast_to([128, 8, 64]), op=ALU.mult),
                         reads=[pb, s8b], writes=[f1])

                def t5_():
                    ga = G_[:, 0, tl, :].unsqueeze(1).broadcast_to([128, 8, 64])
                    gb0 = G_[:, 1, tl, 0:32].unsqueeze(1).broadcast_to([128, 8, 32])
                    gb1 = G_[:, 1, tl, 32:64].unsqueeze(1).broadcast_to([128, 8, 32])
                    S.op("dve", lambda e: e.tensor_tensor(out=rr(f0), in0=rr(f1), in1=ga, op=ALU.mult), reads=[f1, G_], writes=[f0])
                    S.op("dve", lambda e: e.tensor_tensor(out=rr(f2)[:, :, 0:32], in0=rr(f1)[:, :, 32:64], in1=gb0, op=ALU.mult),
                         reads=[f1, G_], writes=[f2])
                    S.op("pool", lambda e: e.tensor_tensor(out=rr(f2)[:, :, 32:64], in0=rr(f1)[:, :, 0:32], in1=gb1, op=ALU.mult),
                         reads=[f1, G_, f2], writes=[f2])

                def t6_():
                    S.op("dve", lambda e: e.tensor_tensor(out=b0[:, :], in0=f0[:, :], in1=f2[:, :], op=ALU.add), reads=[f0, f2], writes=[b0])

                def t7_():
                    for pr in range(4):
                        S.op("pe", lambda e, pr=pr: e.transpose(tpb[:, pr, :], b0[:, pr * 128:(pr + 1) * 128], ident[:, :]),
                             reads=[b0, ident], writes=[tpb], sig=(pr == 3))

                def t8_():
                    if permute:
                        src = tpb[:, 0:4, :].rearrange("p a (n c) -> p a c n", c=4)
                        dsta = dst[:, :, :].rearrange("p a (c t n) -> p a c t n", c=4, t=4)[:, :, :, tl, :]
                    else:
                        src = tpb[:, 0:4, :]
                        dsta = dst[:, :, tl * 128:(tl + 1) * 128]
                    S.op("act", lambda e: e.activation(out=dsta, in_=src, func=AF.Copy), reads=[tpb], writes=[dst])

                return [t0, t1_, t2_, t3_, t4_, t5_, t6_, t7_, t8_]

            def va_chain(B, tl, slot):
                sc = sets[slot]
                yv, svv, y_n, yst = sc["f1"], sc["f2"], sc["b0"], sc["yst"]
                xb = xnT[B % 2]
                st = {}

                def t0():
                    pb = st["pb"] = pmain[slot]
                    for kc in range(8):
                        S.op("pe", lambda e, kc=kc: e.matmul(pb[:, :], lhsT=xb[:, kc, tl * 128:(tl + 1) * 128], rhs=Wb[:, kc, 2560:3072],
                                                             start=(kc == 0), stop=(kc == 7), skip_group_check=True),
                             reads=[Wb, xb], writes=[pb], sig=(kc == 7))
                    S.op("pool", lambda e: e.memset(yst[:, :], 0.0), writes=[yst])

                def t1_():
                    pb = st["pb"]
                    S.op("act", lambda e: e.activation(out=yv[:, :], in_=pb[:, :], func=AF.Gelu_apprx_tanh, accum_out=yst[:, 0:1]),
                         reads=[pb, yst], writes=[yv, yst])

                def t2_():
                    S.op("dve", lambda e: e.tensor_scalar(yst[:, 1:2], yst[:, 0:1], -1.0 / 512, None, ALU.mult), reads=[yst], writes=[yst])

                def t3_():
                    S.op("act", lambda e: e.activation(out=xsq[:, 0:512], in_=yv[:, :], func=AF.Square, bias=yst[:, 1:2], accum_out=yst[:, 2:3]),
                         reads=[yv, yst], writes=[xsq, yst])

                def t4_():
                    S.op("dve", lambda e: e.tensor_scalar(yst[:, 3:4], yst[:, 2:3], 1.0 / 512, EPS, ALU.mult, ALU.add), reads=[yst], writes=[yst])

                def t5_():
                    S.op("pool", lambda e: e.tensor_tensor(out=yst[:, 3:4], in0=yst[:, 3:4], in1=mhalf[:, 0:1], op=ALU.pow),
                         reads=[yst, mhalf], writes=[yst])

                def t6_():
                    S.op("dve", lambda e: e.tensor_scalar(y_n[:, :], yv[:, :], yst[:, 1:2], yst[:, 3:4], ALU.add, ALU.mult),
                         reads=[yv, yst], writes=[y_n])

                def t7_():
                    pb2 = st["pb2"] = paux[slot]
                    for g in range(4):
                        S.op("pe", lambda e, g=g: e.matmul(pb2[:, g * 128:(g + 1) * 128], lhsT=y_n[:, g * 128:(g + 1) * 128], rhs=wsb[:, g, :],
                                                           start=True, stop=True, skip_group_check=True),
                             reads=[y_n, wsb], writes=[pb2], sig=(g == 3))

                def t8_():
                    pb2 = st["pb2"]
                    sv3 = svv[:, :].rearrange("p (g t) -> p g t", g=4)
                    for g in range(4):
                        S.op("dve", lambda e, g=g: e.scalar_tensor_tensor(out=sv3[:, g, :], in0=pb2[:, g * 128:(g + 1) * 128],
                                                                          scalar=lng[:, g:g + 1], in1=biasT[:, g, :], op0=ALU.mult, op1=ALU.add),
                             reads=[pb2, lng, biasT, svv], writes=[svv])

                def t9_():
                    own0 = (B - 4) * 512
                    dsta = gmT[:, :, own0:own0 + 512].rearrange("p g (c t n) -> p g c t n", c=4, t=4)[:, :, :, tl, :]
                    in0 = svv[:, :].rearrange("p (g n c) -> p g c n", g=4, c=4)
                    in1 = w1[:, :, tl * 128:(tl + 1) * 128].rearrange("p g (n c) -> p g c n", c=4)
                    S.op("dve", lambda e: e.tensor_tensor(out=dsta, in0=in0, in1=in1, op=ALU.mult), reads=[svv, w1], writes=[gmT])

                return [t0, t1_, t2_, t3_, t4_, t5_, t6_, t7_, t8_, t9_]

            def stage1_chain(ti, slot):
                B, tl = ti // 4, ti % 4
                x_, s_, x_n, tp, xb = xt[ti % 3], st1[ti % 3], xn[ti % 2], tpx[ti % 2], xnT[B % 2]

                def t0():
                    S.op("pool", lambda e: e.memset(s_[:, :], 0.0), writes=[s_])
                    S.op("act", lambda e: e.activation(out=xsq[:, :], in_=x_[:, :], func=AF.Square, accum_out=s_[:, 0:1]),
                         reads=[x_, s_], writes=[xsq, s_])

                def t1_():
                    S.op("dve", lambda e: e.tensor_scalar(s_[:, 1:2], s_[:, 0:1], 1.0 / 1024, EPS, ALU.mult, ALU.add), reads=[s_], writes=[s_])

                def t2_():
                    S.op("pool", lambda e: e.tensor_tensor(out=s_[:, 2:3], in0=s_[:, 1:2], in1=mhalf[:, 0:1], op=ALU.pow),
                         reads=[s_, mhalf], writes=[s_])

                def t3_():
                    S.op("act", lambda e: e.activation(out=x_n[:, :], in_=x_[:, :], func=AF.Copy, scale=s_[:, 2:3]), reads=[x_, s_], writes=[x_n])
                    if ti + 3 < 48:
                        load_x(ti + 3)

                def t4_():
                    for kc in range(8):
                        S.op("pe", lambda e, kc=kc: e.transpose(tp[:, kc, :], x_n[:, kc * 128:(kc + 1) * 128], ident[:, :]),
                             reads=[x_n, ident], writes=[tp], sig=(kc == 7))

                def t5_():
                    S.op("act", lambda e: e.activation(out=xb[:, :, tl * 128:(tl + 1) * 128], in_=tp[:, :, :], func=AF.Copy), reads=[tp], writes=[xb])

                return [t0, t1_, t2_, t3_, t4_, t5_]

            def run_chains(makers, W):
                pending = list(makers)
                active = []
                free = list(range(W))
                while pending or active:
                    while pending and free:
                        slot = free.pop(0)
                        active.append([slot, pending.pop(0)(slot), 0])
                    for a in list(active):
                        a[1][a[2]]()
                        a[2] += 1
                        if a[2] == len(a[1]):
                            active.remove(a)
                            free.append(a[0])

            def rope_tables(B):
                halo = B < 4
                C1 = 6.28125
                C2 = 2 * PI - 6.28125
                S.op("dve", lambda e: e.tensor_tensor(out=ang[:, :, :], in0=invf[:, :].unsqueeze(1).broadcast_to([128, 4, 32]),
                                                      in1=posf[:, 4 * B:4 * B + 4].unsqueeze(2).broadcast_to([128, 4, 32]), op=ALU.mult),
                     reads=[invf, posf, ang], writes=[ang])
                S.op("dve", lambda e: e.tensor_scalar(ni[:, :, :], ang[:, :, :], 1.0 / (2 * PI), None, ALU.mult), reads=[ang, ni], writes=[ni])
                S.op("dve", lambda e: e.tensor_copy(nf[:, :, :], ni[:, :, :]), reads=[ni, nf], writes=[nf])
                S.op("dve", lambda e: e.scalar_tensor_tensor(out=rr_[:, :, :], in0=nf[:, :, :], scalar=-C1, in1=ang[:, :, :],
                                                             op0=ALU.mult, op1=ALU.add), reads=[nf, ang, rr_], writes=[rr_])
                S.op("dve", lambda e: e.scalar_tensor_tensor(out=rr_[:, :, :], in0=nf[:, :, :], scalar=-C2, in1=rr_[:, :, :],
                                                             op0=ALU.mult, op1=ALU.add), reads=[nf, rr_], writes=[rr_])
                S.op("dve", lambda e: e.tensor_scalar(nf[:, :, :], rr_[:, :, :], PI, 2 * PI, ALU.is_gt, ALU.mult), reads=[rr_, nf], writes=[nf])
                S.op("dve", lambda e: e.tensor_tensor(out=arg[:, 0, :, :], in0=rr_[:, :, :], in1=nf[:, :, :], op=ALU.subtract),
                     reads=[rr_, nf, arg], writes=[arg])
                S.op("dve", lambda e: e.tensor_scalar(rr_[:, :, :], arg[:, 0, :, :], 0.5 * PI, None, ALU.add), reads=[arg, rr_], writes=[rr_])
                S.op("dve", lambda e: e.tensor_scalar(nf[:, :, :], rr_[:, :, :], PI, 2 * PI, ALU.is_gt, ALU.mult), reads=[rr_, nf], writes=[nf])
                S.op("dve", lambda e: e.tensor_tensor(out=arg[:, 1, :, :], in0=rr_[:, :, :], in1=nf[:, :, :], op=ALU.subtract),
                     reads=[rr_, nf, arg], writes=[arg])
                S.op("act", lambda e: e.activation(out=cs[:, :, :, :], in_=arg[:, :, :, :], func=AF.Sin), reads=[arg, cs], writes=[cs])
                for (G_, g1, g2, need) in ((Gq, gq1, gq2, not halo), (Gk, gk1, gk2, True)):
                    if not need:
                        continue
                    cos2 = cs[:, 1, :, :].unsqueeze(2).broadcast_to([128, 4, 2, 32])
                    sin2 = cs[:, 0, :, :].unsqueeze(2).broadcast_to([128, 4, 2, 32])
                    S.op("dve", lambda e, G_=G_, g1=g1, cos2=cos2: e.tensor_tensor(
                        out=G_[:, 0, :, :].rearrange("p t (a d) -> p t a d", a=2), in0=cos2,
                        in1=g1[:, :].rearrange("p (a d) -> p a d", a=2).unsqueeze(1).broadcast_to([128, 4, 2, 32]),
                        op=ALU.mult), reads=[cs, g1, G_], writes=[G_])
                    S.op("dve", lambda e, G_=G_, g2=g2, sin2=sin2: e.tensor_tensor(
                        out=G_[:, 1, :, :].rearrange("p t (a d) -> p t a d", a=2), in0=sin2,
                        in1=g2[:, :].rearrange("p (a d) -> p a d", a=2).unsqueeze(1).broadcast_to([128, 4, 2, 32]),
                        op=ALU.mult), reads=[cs, g2, G_], writes=[G_])

            load_x(0)
            load_x(1)
            load_x(2)
            run_chains([(lambda slot, ti=ti: stage1_chain(ti, slot)) for ti in range(4)], 2)
            for B in range(12):
                halo = B < 4
                xb = xnT[B % 2]
                rope_tables(B)
                vs_ = vst[B % 2]
                gs_ = gst[B % 2]
                ks_ = kst[B % 2]
                qs_ = qst[B % 2]
                fm = [("v", 1024, vs_)]
                if not halo:
                    fm += [("gb", 1536, gs_), ("ga", 3072, gaT), ("u", 2048, uT)]
                for (kind, col0, dstT) in fm:
                    for j in range(4):
                        pb = next_pm()
                        for kc in range(8):
                            S.op("pe", lambda e, pb=pb, kc=kc, xb=xb, c0=col0 + j * 128: e.matmul(
                                pb[:, :], lhsT=Wb[:, kc, c0:c0 + 128], rhs=xb[:, kc, :], start=(kc == 0), stop=(kc == 7),
                                skip_group_check=True), reads=[Wb, xb], writes=[pb], sig=(kc == 7))
                        if kind == "gb":
                            src = pb[:, :].rearrange("p (n c) -> p c n", c=4)
                            dsta = dstT[:, j, :].rearrange("p (c n) -> p c n", c=4)
                        else:
                            src = pb[:, :]
                            dsta = dstT[:, j, :]
                        if kind == "v":
                            S.op("act", lambda e, src=src, dsta=dsta: e.activation(out=dsta, in_=src, func=AF.Copy), reads=[pb], writes=[dstT])
                        else:
                            fn = AF.Gelu_apprx_tanh if kind == "u" else AF.Silu
                            S.op("act", lambda e, src=src, dsta=dsta, fn=fn: e.activation(out=dsta, in_=src, func=fn),
                                 reads=[pb], writes=[dstT])
                if not halo:
                    S.op("pool", lambda e: e.tensor_tensor(out=w1[:, :, :], in0=uT[:, :, :], in1=gaT[:, :, :], op=ALU.mult),
                         reads=[uT, gaT], writes=[w1])

                makers = []
                for tl in range(4):
                    makers.append(lambda slot, tl=tl, B=B: qk_chain("k", B, tl, slot))
                    if not halo:
                        makers.append(lambda slot, tl=tl, B=B: qk_chain("q", B, tl, slot))
                        makers.append(lambda slot, tl=tl, B=B: va_chain(B, tl, slot))
                    if B + 1 < 12:
                        makers.append(lambda slot, ti=4 * (B + 1) + tl: stage1_chain(ti, slot))
                run_chains(makers, 5 if halo else NSET)

                c0 = B * 512
                ssem = stsem[B % 2]
                S.group(ssem, 2 if halo else 4)
                S.dma(ssem, lambda e, ks_=ks_, c0=c0: e.dma_start(out=kT_d[:, :, c0:c0 + 512].rearrange("a p l -> p a l"),
                                                                 in_=ks_[:, :, :]), reads=[ks_])
                S.dma(ssem, lambda e, vs_=vs_, c0=c0: e.dma_start(out=vT_d[:, :, c0:c0 + 512].rearrange("a p l -> p a l"),
                                                                 in_=vs_[:, :, :]), reads=[vs_])
                if not halo:
                    o0 = c0 - NHALO
                    S.dma(ssem, lambda e, qs_=qs_, o0=o0: e.dma_start(out=qT_d[:, :, o0:o0 + 512].rearrange("a p l -> p a l"),
                                                                     in_=qs_[:, :, :]), reads=[qs_])
                    S.dma(ssem, lambda e, gs_=gs_, o0=o0: e.dma_start(out=gbT_d[:, :, o0:o0 + 512].rearrange("a p l -> p a l"),
                                                                     in_=gs_[:, :, :]), reads=[gs_])
            S.barrier()
            with nc.Block() as block:
                S.emit(block)

        if stop == "A":
            return nc
        with ExitStack() as BC:
            atT = sb(BC, "atT", [128, 4, NTOK], BF)
            Wo = sb(BC, "Wo", [128, 8, 1024], BF)
            Wg = sb(BC, "Wg", [128, 8, 1024], BF)
            Wp = sb(BC, "Wp", [128, 2, 1024], BF)
            gple = sb(BC, "gple", [128, 8], F32)
            bg_b = sb(BC, "bg_b", [1, 1024], BF)
            ones1 = sb(BC, "ones1", [1, 128], BF)
            ld(gple, gple[:, :], gple_d[:, :])
            S.op("dve", lambda e: e.memset(ones1[:, :], 1.0), writes=[ones1])

            with ExitStack() as Bs:
                wst2 = [sb(Bs, "wst2_0", [128, 1024], F32)] * 2
                ld(wst2[0], wst2[0][0:1, :], bgate_d[:, :])
                S.op("dve", lambda e: e.tensor_copy(bg_b[:, :], wst2[0][0:1, :]), reads=[wst2[0]], writes=[bg_b])
                w2sem = [S.new_dma_sem() for _ in range(2)]
                jobs = [(w_out, Wo, kc, None) for kc in range(8)] + [(w_gate, Wg, kc, gple) for kc in range(8)] + \
                       [(w_pp, Wp, kc, None) for kc in range(2)]
                jcnt = [0]

                def tail_weight_jobs(n):
                    for _ in range(n):
                        if not jobs:
                            return
                        (src, dstW, kc, gsc) = jobs.pop(0)
                        cnt = jcnt[0]
                        jcnt[0] += 1
                        stg = wst2[cnt % 2]
                        S.dma(w2sem[cnt % 2], lambda e, stg=stg, src=src, kc=kc: e.dma_start(out=stg[:, :], in_=src[kc * 128:(kc + 1) * 128, :]),
                              writes=[stg])
                        if gsc is None:
                            S.op("dve", lambda e, stg=stg, dstW=dstW, kc=kc: e.tensor_copy(dstW[:, kc, :], stg[:, :]),
                                 reads=[stg], writes=[dstW])
                        else:
                            S.op("dve", lambda e, stg=stg, dstW=dstW, kc=kc, gsc=gsc: e.tensor_scalar(
                                dstW[:, kc, :], stg[:, :], gsc[:, kc:kc + 1], None, ALU.mult), reads=[stg, gsc], writes=[dstW])

                qs = [sb(Bs, "qs%d" % i, [128, 2048], BF) for i in range(2)]
                ksb = [sb(Bs, "ksb%d" % i, [128, 4096], BF) for i in range(2)]
                vsb = [sb(Bs, "vsb%d" % i, [128, 4096], BF) for i in range(2)]
                gsb = [sb(Bs, "gsb%d" % i, [128, 2048], BF) for i in range(2)]
                Vc = sb(Bs, "Vc", [128, 69, 192], BF)
                LA = 4
                NST = 4
                LAS = 1
                NPT = 6
                pt = [sb(Bs, "pt%d" % i, [128, 512], BF) for i in range(NPT)]
                rec = sb(Bs, "rec", [128, 2048], F32)
                accm = [ps(Bs, "accm%d" % i, [128, 512], F32) for i in range(2)]
                acc16p = ps(Bs, "acc16p", [128, 512], F32)
                a16 = sb(Bs, "a16", [128, 2048], F32)
                stp = [ps(Bs, "stp%d" % i, [128, 512], F32) for i in range(LA + 1)]
                vtp = [TV(stp[i], (lambda i=i: stp[i][:, :].bitcast(BF).rearrange("p (a t) -> p a t", a=8))) for i in range(LA + 1)]
                bsem = [S.new_dma_sem() for _ in range(2)]

                S.op("pool", lambda e: e.memset(Vc[:, :, 64:128], 1.0), writes=[Vc])

                def chunk(ten, rs, kind, sbk, a, b=0):
                    base = ten[rs, sbk * 2048:(sbk + 1) * 2048]
                    if kind == 0:
                        return base.rearrange("p (k c i x) -> p k c i x", k=4, c=4, x=4)[:, :, a % 4, :, a // 4]
                    if kind == 1:
                        return base[:, 512 * b + 128 * a:512 * b + 128 * a + 128]
                    k, q4 = a // 4, a % 4
                    return base[:, 512 * k:512 * k + 512].rearrange("p (c q n) -> p c q n", c=4, q=4)[:, :, q4, :]

                def kchunk(ten, rs, kind, sbk, a, b=0):
                    if kind == 0:
                        st0 = sbk * 2048 + a
                        return ten[rs, st0:st0 + 16 * 127 + 1:16]
                    if kind == 1:
                        st0 = sbk * 2048 + 512 * b + a
                        return ten[rs, st0:st0 + 4 * 127 + 1:4]
                    st0 = sbk * 2048 + 128 * a
                    return ten[rs, st0:st0 + 128]

                def bank_of(kind, a, b):
                    return b if kind == 1 else a // 4

                def accchunk(rs, kind, a, b=0):
                    t_ = accm[bank_of(kind, a, b) % 2]
                    if kind == 1:
                        return t_[rs, 128 * a:128 * a + 128]
                    q4 = a % 4
                    return t_[rs, :].rearrange("p (c q n) -> p c q n", c=4, q=4)[:, :, q4, :]

                blocks = []
                for c in range(16):
                    blocks.append((0, c, 0, c, (0, c, 0, 48 + c)))
                for bk in range(4):
                    for c4 in range(4):
                        bb = bk
                        prev = (0, c4, 3, 64 + c4) if bb == 0 else (1, c4, bb - 1, 16 + c4 * 4 + bb - 1)
                        blocks.append((1, c4, bb, 16 + c4 * 4 + bb, prev))
                    for b1 in range(4 * bk, 4 * bk + 4):
                        prev = (0, 15, 0, 68) if b1 == 0 else (1, b1 - 1, 0, 32 + b1 - 1)
                        blocks.append((2, b1, 0, 32 + b1, prev))
                vlist = []
                for c in range(16):
                    vlist.append((c, 1, 0, c, 0))
                    vlist.append((48 + c, 0, 0, c, 0))
                for c4 in range(4):
                    for bb in range(4):
                        vlist.append((16 + c4 * 4 + bb, 1, 1, c4, bb))
                    vlist.append((64 + c4, 0, 1, c4, 3))
                for b1 in range(16):
                    vlist.append((32 + b1, 1, 2, b1, 0))
                vlist.append((68, 0, 2, 15, 0))
                vlist.sort()

                it = 0
                vtpi = 0
                sti = 0
                pti = 0
                for Sb in (() if skipB else (1, 2)):
                    if Sb == 1:
                        S.op("pool", lambda e: e.tensor_scalar(Vc[:, 48:69, 64:128], Vc[:, 48:69, 64:128], flag[:, 0:1], None, ALU.mult),
                             reads=[flag, Vc], writes=[Vc])
                    else:
                        S.op("pool", lambda e: e.memset(Vc[:, 48:69, 64:128], 1.0), reads=[Vc], writes=[Vc])
                    for p in range(4):
                        bi = it % 2
                        q_, k_, v_, g_ = qs[bi], ksb[bi], vsb[bi], gsb[bi]
                        o0 = (Sb - 1) * 2048
                        S.group(bsem[bi], 4)
                        S.dma(bsem[bi], lambda e, q_=q_, p=p, o0=o0: e.dma_start(out=q_[:, :], in_=qT_d[p, :, o0:o0 + 2048]), writes=[q_])
                        S.dma(bsem[bi], lambda e, k_=k_, p=p, o0=o0: e.dma_start(out=k_[:, :], in_=kT_d[p, :, o0:o0 + 4096]), writes=[k_])
                        S.dma(bsem[bi], lambda e, v_=v_, p=p, o0=o0: e.dma_start(out=v_[:, :], in_=vT_d[p, :, o0:o0 + 4096]), writes=[v_])
                        S.dma(bsem[bi], lambda e, g_=g_, p=p, o0=o0: e.dma_start(out=g_[:, :], in_=gbT_d[p, :, o0:o0 + 2048]), writes=[g_])
                        for i0 in range(0, 69, 8):
                            n = min(8, 69 - i0)
                            tpv = vtp[vtpi % (LA + 1)]
                            vtpi += 1
                            for jx in range(n):
                                (idx, sbk, kind, a, b) = vlist[i0 + jx]
                                assert idx == i0 + jx
                                src = kchunk(v_, slice(0, 128), kind, sbk, a, b)
                                S.op("pe", lambda e, tpv=tpv, jx=jx, src=src: e.transpose(tpv[:, jx, :], src, ident[:, :]),
                                     reads=[v_, ident], writes=[tpv], sig=(jx == n - 1))
                            S.op("dve", lambda e, tpv=tpv, i0=i0, n=n: e.tensor_copy(Vc[:, i0:i0 + n, 0:64], tpv[:, 0:n, 0:64]),
                                 reads=[tpv], writes=[Vc])
                            S.op("act", lambda e, tpv=tpv, i0=i0, n=n: e.activation(out=Vc[:, i0:i0 + n, 128:192], in_=tpv[:, 0:n, 64:128], func=AF.Copy),
                                 reads=[tpv, Vc], writes=[Vc])
                        for hp in range(2):
                            tail_weight_jobs(2)
                            rs = slice(hp * 64, hp * 64 + 64)
                            lo = 0 if hp == 0 else 64
                            first = True
                            pend = []
                            npair = len(blocks) // 2

                            def do_pv(pair, ptile, sig_last, extra_reads=()):
                                for j in range(2):
                                    (kind, a, b, cidx, prev) = blocks[2 * pair + j]
                                    (psb, pa, pb_, pidx) = prev
                                    for half, vidx in ((0, pidx), (1, cidx)):
                                        lw = Vc[:, vidx, lo:lo + 128]
                                        rhs_full = ptile[:, j * 256 + half * 128: j * 256 + half * 128 + 128]
                                        if kind == 0:
                                            o16 = acc16p[:, (a % 4) * 128:(a % 4) * 128 + 128]
                                            S.op("pe", lambda e, lw=lw, o16=o16, rhs_full=rhs_full, half=half: e.matmul(
                                                o16, lhsT=lw, rhs=rhs_full, start=(half == 0), stop=(half == 1), skip_group_check=True),
                                                reads=[Vc, ptile] + list(extra_reads), writes=[acc16p], sig=(sig_last and j == 1 and half == 1))
                                            if half == 1 and a % 4 == 3:
                                                S.op("act", lambda e, a=a: e.activation(out=a16[:, (a - 3) * 128:(a + 1) * 128], in_=acc16p[:, :],
                                                                                      func=AF.Copy), reads=[acc16p], writes=[a16])
                                        else:
                                            oa = accchunk(slice(0, 128), kind, a, b)
                                            bk_ = bank_of(kind, a, b)
                                            am = accm[bk_ % 2]
                                            rr = rhs_full if kind == 1 else rhs_full.rearrange("p (x y) -> p x y", x=4)
                                            st_ = (kind == 1 and a == 0 and half == 0)
                                            S.op("pe", lambda e, lw=lw, oa=oa, rr=rr, st_=st_: e.matmul(oa, lhsT=lw, rhs=rr, start=st_, stop=False,
                                                                                                      skip_group_check=True),
                                                 reads=[Vc, ptile] + list(extra_reads), writes=[am], sig=(sig_last and j == 1 and half == 1))
                                            if kind == 2 and a % 4 == 3 and half == 1:
                                                S.op("act", lambda e, am=am, bk_=bk_: e.activation(out=rec[:, 512 * bk_:512 * bk_ + 512], in_=am[:, :],
                                                                                                  func=AF.Copy), reads=[am], writes=[rec])

                            for ss in range(npair // 2):
                                stbs = []
                                for pi_ in range(2):
                                    pair = 2 * ss + pi_
                                    stb = stp[sti % NST]
                                    sti += 1
                                    stbs.append(stb)
                                    for j in range(2):
                                        (kind, a, b, cidx, prev) = blocks[2 * pair + j]
                                        (psb, pa, pb_, pidx) = prev
                                        qa = chunk(q_, rs, kind, 0, a, b)
                                        kprev = kchunk(k_, rs, kind, psb, pa, pb_)
                                        kcur = kchunk(k_, rs, kind, 1, a, b)
                                        S.op("pe", lambda e, stb=stb, j=j, kprev=kprev, qa=qa: e.matmul(
                                            stb[:, j * 256:j * 256 + 128], lhsT=kprev, rhs=qa, start=True, stop=True, skip_group_check=True),
                                            reads=[k_, q_], writes=[stb], sig=False)
                                        last = (pi_ == 1 and j == 1)
                                        S.op("pe", lambda e, stb=stb, j=j, kcur=kcur, qa=qa: e.matmul(
                                            stb[:, j * 256 + 128:j * 256 + 256], lhsT=kcur, rhs=qa, start=True, stop=True, skip_group_check=True),
                                            reads=[k_, q_], writes=(list(stbs) if last else [stb]), sig=last)
                                ptiles = []
                                for pi_ in range(2):
                                    pair = 2 * ss + pi_
                                    stb = stbs[pi_]
                                    ptile = pt[pti % NPT]
                                    pti += 1
                                    ptiles.append(ptile)
                                    S.op("act", lambda e, stb=stb, ptile=ptile: e.activation(out=ptile[:, :], in_=stb[:, :], func=AF.Exp, scale=0.125),
                                         reads=[stb], writes=[ptile])
                                    kind = blocks[2 * pair][0]
                                    mk = masks[:, 2 * kind:2 * kind + 2, :].unsqueeze(1).broadcast_to([128, 2, 2, 128])
                                    pv4 = ptile[:, :].rearrange("p (j h n) -> p j h n", j=2, h=2)
                                    meng = "dve"
                                    S.op(meng, lambda e, pv4=pv4, mk=mk: e.tensor_tensor(out=pv4, in0=pv4, in1=mk, op=ALU.mult),
                                         reads=[ptile, masks], writes=[ptile])
                                pend.append((ss, ptiles))
                                if len(pend) > LAS:
                                    ss0, pts = pend.pop(0)
                                    do_pv(2 * ss0, pts[0], False)
                                    do_pv(2 * ss0 + 1, pts[1], True, extra_reads=[pts[0]])
                            while pend:
                                ss0, pts = pend.pop(0)
                                do_pv(2 * ss0, pts[0], False)
                                do_pv(2 * ss0 + 1, pts[1], True, extra_reads=[pts[0]])
                            num = slice(0, 64) if hp == 0 else slice(64, 128)
                            den = slice(64, 128) if hp == 0 else slice(0, 64)
                            a16v = a16[:, :].rearrange("p (cc c k i) -> p k c i cc", cc=4, c=4, k=4)
                            recv = rec[:, :].rearrange("p (k c i cc) -> p k c i cc", k=4, c=4, cc=4)
                            for kk in range(4):
                                S.op("dve", lambda e, kk=kk: e.tensor_tensor(out=recv[:, kk], in0=recv[:, kk], in1=a16v[:, kk], op=ALU.add),
                                     reads=[rec, a16], writes=[rec])
                            S.op("act", lambda e, num=num, den=den: e.activation(out=a16[num, :], in_=rec[den, :], func=AF.Ln), reads=[rec], writes=[a16])
                            S.op("act", lambda e, num=num: e.activation(out=a16[num, :], in_=a16[num, :], func=AF.Exp, scale=-1.0), reads=[a16], writes=[a16])
                            S.op("dve", lambda e, num=num: e.tensor_tensor(out=rec[num, :], in0=rec[num, :], in1=a16[num, :], op=ALU.mult),
                                 reads=[rec, a16], writes=[rec])
                            S.op("pool", lambda e, num=num, g_=g_, p=p, o0=o0: e.tensor_tensor(out=atT[num, p, o0:o0 + 2048], in0=rec[num, :],
                                                                                          in1=g_[num, :], op=ALU.mult),
                                 reads=[rec, g_], writes=[atT])
                        it += 1
                if dbg:
                    at_dbg = nc.dram_tensor("at_dbg", [128, 4, NTOK], BF, kind="ExternalOutput").ap()
                    gm_dbg = nc.dram_tensor("gm_dbg", [128, 4, NTOK], BF, kind="ExternalOutput").ap()
                    S.dma(dsem_c, lambda e: e.dma_start(out=at_dbg[:, :, :], in_=atT[:, :, :]), reads=[atT])
                    S.dma(dsem_c, lambda e: e.dma_start(out=gm_dbg[:, :, :], in_=gmT[:, :, :]), reads=[gmT])
                S.barrier()
                with nc.Block() as block:
                    S.emit(block)

            if stop == "B":
                return nc
            with ExitStack() as Cs:
                NS = 2
                xt2 = [sb(Cs, "xt2_%d" % i, [128, 1024], F32) for i in range(NS)]
                pt2 = [sb(Cs, "pt2_%d" % i, [128, 256], F32) for i in range(NS)]
                pbf = [sb(Cs, "pbf%d" % i, [128, 256], BF) for i in range(NS)]
                pT = [sb(Cs, "pT%d" % i, [128, 2, 128], BF) for i in range(NS)]
                h_ = [sb(Cs, "h%d" % i, [128, 1024], F32) for i in range(NS)]
                hsq = sb(Cs, "hsq", [128, 1024], BF)
                hst = [sb(Cs, "hst%d" % i, [128, 4], F32) for i in range(NS)]
                hn = [sb(Cs, "hn%d" % i, [128, 1024], BF) for i in range(NS)]
                hnT = [sb(Cs, "hnT%d" % i, [128, 8, 128], BF) for i in range(NS)]
                gate = [sb(Cs, "gate%d" % i, [128, 1024], F32) for i in range(NS)]
                ot = [sb(Cs, "ot%d" % i, [128, 1024], F32) for i in range(NS)]
                hp_ = [[ps(Cs, "hp%d_%d" % (i, g), [128, 512], F32) for g in range(2)] for i in range(NS)]
                gp_ = [[ps(Cs, "gp%d_%d" % (i, g), [128, 512], F32) for g in range(2)] for i in range(NS)]
                tpv8 = [[TV(gp_[i][g], (lambda i=i, g=g: gp_[i][g][:, :].bitcast(BF).rearrange("p (a t) -> p a t", a=8)))
                         for g in range(2)] for i in range(NS)]
                csem = [S.new_dma_sem() for _ in range(NS)]
                osem = [S.new_dma_sem() for _ in range(NS)]

                tiles = [(Sb, k, w) for Sb in (0, 1) for k in range(4) for w in range(4)]

                def rows(Sb, k, w):
                    return Sb * 2048 + 512 * k + w

                def load_c(i, slot):
                    (Sb, k, w) = tiles[i]
                    x_ = xt2[slot]
                    p_ = pt2[slot]
                    S.group(csem[slot], 2)
                    r0 = rows(Sb, k, w)
                    S.dma(csem[slot], lambda e, x_=x_, r0=r0: e.dma_start(
                        out=x_[:, :], in_=xh[NHALO + r0:NHALO + r0 + 509:4, :]), writes=[x_])
                    S.dma(csem[slot], lambda e, p_=p_, r0=r0: e.dma_start(
                        out=p_[:, :], in_=pown[r0:r0 + 509:4, :]), writes=[p_])

                def tail_chain(i, slot):
                    (Sb, k, w) = tiles[i]
                    L0 = Sb * 2048 + 512 * k + 128 * w
                    x_, p_, hh, hs, h_n, h_nT = xt2[slot], pt2[slot], h_[slot], hst[slot], hn[slot], hnT[slot]
                    pb_, p_T, gt, oo = pbf[slot], pT[slot], gate[slot], ot[slot]
                    hp, gp, tp = hp_[slot], gp_[slot], tpv8[slot]

                    def t0():
                        for g in range(2):
                            for e8 in range(8):
                                srcT = atT if e8 < 4 else gmT
                                S.op("pe", lambda e, g=g, e8=e8, srcT=srcT: e.matmul(
                                    hp[g][:, :], lhsT=srcT[:, e8 % 4, L0:L0 + 128], rhs=Wo[:, e8, g * 512:(g + 1) * 512],
                                    start=(e8 == 0), stop=(e8 == 7), skip_group_check=True),
                                    reads=[atT, gmT, Wo], writes=([hp[0], hp[1]] if g == 1 else [hp[g]]), sig=(g == 1 and e8 == 7))

                    def t1():
                        for g in range(2):
                            S.op("dve", lambda e, g=g: e.tensor_tensor(out=hh[:, g * 512:(g + 1) * 512], in0=hp[g][:, :],
                                                                    in1=x_[:, g * 512:(g + 1) * 512], op=ALU.add),
                                 reads=[hp[g], x_, hh], writes=[hh])
                        S.op("pool", lambda e: e.memset(hs[:, :], 0.0), writes=[hs])
                        S.op("pool", lambda e: e.tensor_copy(pb_[:, :], p_[:, :]), reads=[p_], writes=[pb_])
                        if i + NS < len(tiles):
                            load_c(i + NS, slot)

                    def t2():
                        S.op("act", lambda e: e.activation(out=hsq[:, :], in_=hh[:, :], func=AF.Square, accum_out=hs[:, 0:1]),
                             reads=[hh, hs], writes=[hsq, hs])
                        for kc in range(2):
                            S.op("pe", lambda e, kc=kc: e.transpose(tp[1][:, kc, :], pb_[:, kc * 128:(kc + 1) * 128], ident[:, :]),
                                 reads=[pb_, ident], writes=[tp[1]], sig=(kc == 1))

                    def t3():
                        S.op("dve", lambda e: e.tensor_scalar(hs[:, 1:2], hs[:, 0:1], 1.0 / 1024, EPS, ALU.mult, ALU.add), reads=[hs], writes=[hs])
                        S.op("act", lambda e: e.activation(out=p_T[:, :, :], in_=tp[1][:, 0:2, :], func=AF.Copy), reads=[tp[1]], writes=[p_T])

                    def t4():
                        S.op("pool", lambda e: e.tensor_tensor(out=hs[:, 2:3], in0=hs[:, 1:2], in1=mhalf[:, 0:1], op=ALU.pow),
                             reads=[hs, mhalf], writes=[hs])

                    def t5():
                        S.op("act", lambda e: e.activation(out=h_n[:, :], in_=hh[:, :], func=AF.Copy, scale=hs[:, 2:3]), reads=[hh, hs], writes=[h_n])

                    def t6():
                        for kc in range(8):
                            S.op("pe", lambda e, kc=kc: e.transpose(tp[0][:, kc, :], h_n[:, kc * 128:(kc + 1) * 128], ident[:, :]),
                                 reads=[h_n, ident], writes=[tp[0]], sig=(kc == 7))

                    def t7():
                        S.op("dve", lambda e: e.tensor_copy(h_nT[:, :, :], tp[0][:, :, :]), reads=[tp[0]], writes=[h_nT])

                    def t8():
                        for g in range(2):
                            for kc in range(8):
                                S.op("pe", lambda e, g=g, kc=kc: e.matmul(gp[g][:, :], lhsT=h_nT[:, kc, :], rhs=Wg[:, kc, g * 512:(g + 1) * 512],
                                                                           start=(kc == 0), stop=False, skip_group_check=True),
                                     reads=[h_nT, Wg], writes=[gp[g]], sig=False)
                            S.op("pe", lambda e, g=g: e.matmul(gp[g][:, :], lhsT=ones1[0:1, :], rhs=bg_b[0:1, g * 512:(g + 1) * 512],
                                                               start=False, stop=True, skip_group_check=True),
                                 reads=[h_nT, Wg, ones1, bg_b], writes=([gp[0], gp[1]] if g == 1 else [gp[g]]), sig=(g == 1))
                        for g in range(2):
                            for kc in range(2):
                                S.op("pe", lambda e, g=g, kc=kc: e.matmul(hp[g][:, :], lhsT=p_T[:, kc, :], rhs=Wp[:, kc, g * 512:(g + 1) * 512],
                                                                           start=(kc == 0), stop=(kc == 1), skip_group_check=True),
                                     reads=[p_T, Wp], writes=([hp[0], hp[1]] if g == 1 else [hp[g]]), sig=(g == 1 and kc == 1))

                    def t9():
                        for g in range(2):
                            S.op("act", lambda e, g=g: e.activation(out=gt[:, g * 512:(g + 1) * 512], in_=gp[g][:, :], func=AF.Sigmoid),
                                 reads=[gp[g], gt], writes=[gt])

                    def t10():
                        for g in range(2):
                            S.op("dve", lambda e, g=g: e.tensor_tensor(out=oo[:, g * 512:(g + 1) * 512], in0=hp[g][:, :],
                                                                      in1=gt[:, g * 512:(g + 1) * 512], op=ALU.mult),
                                 reads=[hp[g], gt, oo], writes=[oo])

                    def t11():
                        S.op("pool", lambda e: e.tensor_tensor(out=oo[:, :], in0=oo[:, :], in1=hh[:, :], op=ALU.add), reads=[oo, hh], writes=[oo])
                        S.group(osem[slot], 1)
                        r0 = rows(Sb, k, w)
                        S.dma(osem[slot], lambda e, r0=r0: e.dma_start(out=out[r0:r0 + 509:4, :], in_=oo[:, :]), reads=[oo])

                    return [t0, t1, t2, t3, t4, t5, t6, t7, t8, t9, t10, t11]

                for i0 in range(NS):
                    load_c(i0, i0)
                run_chains_g([(lambda slot, i=i: tail_chain(i, slot)) for i in range(len(tiles))], NS)
                S.barrier()
                with nc.Block() as block:
                    S.emit(block)
    return nc


_CACHE = {}


def kernel(x, p, positions, g_in, w_in, q_norm, k_norm, w_spatial, b_spatial,
           ln_v_g, ln_v_b, w_out, g_ple, w_ple_gate, b_ple_gate, w_ple_proj):
    f32 = np.float32
    x = np.asarray(x, f32)
    p = np.asarray(p, f32)[0]
    positions = np.asarray(positions, np.int32)
    ident, masks, invf, tril = _const_tables()
    if "nc" not in _CACHE:
        _CACHE["nc"] = build_nc()
    nc = _CACHE["nc"]

    def col8(v):
        return np.ascontiguousarray(np.asarray(v, f32).reshape(8, 128).T)

    shared = {
        "w_in": np.ascontiguousarray(np.asarray(w_in, f32)[0]),
        "w_out": np.ascontiguousarray(np.asarray(w_out, f32)[0]),
        "w_gate": np.ascontiguousarray(np.asarray(w_ple_gate, f32)[0]),
        "w_pp": np.ascontiguousarray(np.asarray(w_ple_proj, f32)[0]),
        "gin": col8(np.asarray(g_in)[0]),
        "gple": col8(np.asarray(g_ple)[0]),
        "qn_b": np.ascontiguousarray(np.broadcast_to(np.asarray(q_norm, f32)[0][None, :], (128, 64))),
        "kn_b": np.ascontiguousarray(np.broadcast_to(np.asarray(k_norm, f32)[0][None, :], (128, 64))),
        "lng": np.ascontiguousarray(np.asarray(ln_v_g, f32)[0].reshape(4, 128).T),
        "lnb": np.ascontiguousarray(np.asarray(ln_v_b, f32)[0].reshape(4, 128).T),
        "bsb": np.ascontiguousarray(np.broadcast_to(np.asarray(b_spatial, f32)[0].reshape(1, 512), (128, 512))),
        "wsT": np.ascontiguousarray(np.asarray(w_spatial, f32)[0].transpose(2, 0, 1).reshape(128, 512)),
        "bgate": np.ascontiguousarray(np.asarray(b_ple_gate, f32)[0].reshape(1, 1024)),
        "ident": ident, "masks": np.ascontiguousarray(masks.reshape(128, 768)), "invf": invf, "tril": tril,
    }
    in_maps = []
    for core in range(8):
        b, half = core // 2, core % 2
        s0 = half * NTOK
        if half == 0:
            xhalo = np.zeros((NHALO, 1024), f32)
            phalo = np.zeros((NHALO,), np.int32)
        else:
            xhalo = x[b, s0 - NHALO:s0]
            phalo = positions[b, s0 - NHALO:s0]
        xh = np.concatenate([xhalo, x[b, s0:s0 + NTOK]], axis=0)
        pos = np.concatenate([phalo, positions[b, s0:s0 + NTOK]], axis=0)
        m = dict(shared)
        m["xh"] = np.ascontiguousarray(xh)
        m["pown"] = np.ascontiguousarray(p[b, s0:s0 + NTOK])
        m["pos_t"] = np.ascontiguousarray(pos.reshape(48, 128).T)
        m["flag"] = np.full((128, 1), float(half), f32)
        in_maps.append(m)
    res = run_bass_kernel_spmd(nc, in_maps, core_ids=list(range(8)))
    outp = np.empty((4, 8192, 1024), f32)
    for core in range(8):
        b, half = core // 2, core % 2
        outp[b, half * NTOK:(half + 1) * NTOK] = res.results[core]["out"]
    return outp
```
